# Optimizing a Trainium2 kernel written in Bass

```python
import jax, jax.numpy as jnp
from jax import lax
import numpy as np

D_MODEL = 1024
BATCH = 16
SEQ = 2048
DEPTH = 2

N_A_LAYERS = DEPTH // 2
N_B_LAYERS = DEPTH - N_A_LAYERS
N_META = 16
EPS = 1e-6
NEG_INF = -1e30

D_FF = 2816

GLA_HEADS = 4
GLA_DK = D_MODEL // 2 // GLA_HEADS
GLA_DV = D_MODEL // GLA_HEADS
GLA_QK = GLA_HEADS * GLA_DK
GLA_V = GLA_HEADS * GLA_DV
GLA_RANK = 16
GLA_TAU = 16.0
GLA_IN = 2 * GLA_QK + 2 * GLA_V + GLA_RANK
CHUNK = 64

N_Q_HEADS = 16
N_KV_HEADS = 4
HEAD_DIM = 64
GROUP = N_Q_HEADS // N_KV_HEADS
WINDOW = 128
ROT_DIM = HEAD_DIM // 4
ROPE_THETA = 500000.0

kernel_name = "yoco_gla_swa_sink_macaron"


def rmsnorm(x, g):
    xf = x.astype(jnp.float32)
    y = xf * lax.rsqrt(jnp.mean(xf * xf, axis=-1, keepdims=True) + EPS)
    return (y * g.astype(jnp.float32)).astype(x.dtype)


def swiglu(x, w_in, w_out):
    gate, up = jnp.split(x @ w_in, 2, axis=-1)
    return (jax.nn.silu(gate) * up) @ w_out


def rope_tables(length):
    inv_freq = ROPE_THETA ** (-jnp.arange(0, ROT_DIM, 2, dtype=jnp.float32) / ROT_DIM)
    ang = jnp.arange(length, dtype=jnp.float32)[:, None] * inv_freq[None, :]
    return jnp.cos(ang), jnp.sin(ang)


def apply_partial_rope(x, cos, sin):
    half = ROT_DIM // 2
    shape = (cos.shape[0],) + (1,) * (x.ndim - 3) + (half,)
    c, s = cos.reshape(shape), sin.reshape(shape)
    x1 = x[..., :half].astype(jnp.float32)
    x2 = x[..., half:ROT_DIM].astype(jnp.float32)
    return jnp.concatenate([(x1 * c - x2 * s).astype(x.dtype), (x2 * c + x1 * s).astype(x.dtype),
                            x[..., ROT_DIM:]], axis=-1)


def gla_mixer(hn, w_in, w_gate, b_gate, g_head, w_out):
    B, L, _ = hn.shape
    q, k, v, r, lr = jnp.split(hn @ w_in, [GLA_QK, 2 * GLA_QK, 2 * GLA_QK + GLA_V, 2 * GLA_QK + 2 * GLA_V], axis=-1)
    gk = jax.nn.log_sigmoid((lr @ w_gate + b_gate).astype(jnp.float32)) / GLA_TAU
    pad = CHUNK - N_META
    padf = lambda t: jnp.pad(t, ((0, 0), (pad, 0), (0, 0)))
    Lp = L + pad
    nC = Lp // CHUNK
    def chunks(t, d):
        return t.reshape(B, nC, CHUNK, GLA_HEADS, d).transpose(0, 3, 1, 2, 4)
    qf = chunks(padf(q).astype(jnp.float32), GLA_DK) * (GLA_DK ** -0.5)
    kf = chunks(padf(k).astype(jnp.float32), GLA_DK)
    vf = chunks(padf(v).astype(jnp.float32), GLA_DV)
    bcum = jnp.cumsum(chunks(padf(gk), GLA_DK), axis=3)
    q_dec = qf * jnp.exp(bcum)
    k_dec = kf * jnp.exp(-bcum)
    causal = jnp.tril(jnp.ones((CHUNK, CHUNK), dtype=bool))
    att = jnp.where(causal, jnp.einsum('bhncd,bhnsd->bhncs', q_dec, k_dec), 0.0)
    o_intra = jnp.einsum('bhncs,bhnsv->bhncv', att, vf)
    b_last = bcum[:, :, :, -1:, :]
    chunk_kv = jnp.einsum('bhnsd,bhnsv->bhndv', kf * jnp.exp(b_last - bcum), vf)
    decay = jnp.exp(b_last[:, :, :, 0, :])

    def step(S, inp):
        kv_c, dec_c = inp
        return dec_c[..., None] * S + kv_c, S
    S0 = jnp.zeros((B, GLA_HEADS, GLA_DK, GLA_DV), jnp.float32)
    _, states = lax.scan(step, S0, (jnp.moveaxis(chunk_kv, 2, 0), jnp.moveaxis(decay, 2, 0)))
    states = jnp.moveaxis(states, 0, 2)
    o = o_intra + jnp.einsum('bhncd,bhndv->bhncv', q_dec, states)
    o = o.reshape(B, GLA_HEADS, Lp, GLA_DV)[:, :, pad:].transpose(0, 2, 1, 3)
    o = rmsnorm(o, g_head) * jax.nn.silu(r.astype(jnp.float32)).reshape(B, L, GLA_HEADS, GLA_DV)
    return o.reshape(B, L, GLA_V).astype(hn.dtype) @ w_out


def shared_kv(h, g_kv, w_kv, cos, sin):
    B, L, _ = h.shape
    k, v = jnp.split(rmsnorm(h, g_kv) @ w_kv, 2, axis=-1)
    k = apply_partial_rope(k.reshape(B, L, N_KV_HEADS, HEAD_DIM), cos, sin)
    v = v.reshape(B, L, N_KV_HEADS, HEAD_DIM)
    return k.transpose(0, 2, 1, 3), v.transpose(0, 2, 1, 3)


def sink_softmax(s, mask, sink):
    s = jnp.where(mask, s.astype(jnp.float32), NEG_INF)
    m = jnp.maximum(jnp.max(s, axis=-1, keepdims=True), sink)
    p = jnp.exp(s - m)
    return p / (jnp.sum(p, axis=-1, keepdims=True) + jnp.exp(sink - m))


def swa_mixer(hn, w_q, sinks, w_out, k, v, cos, sin):
    B, L, _ = hn.shape
    S = L - N_META
    nB = S // WINDOW
    q = (hn @ w_q).reshape(B, L, N_KV_HEADS, GROUP, HEAD_DIM)
    q = (apply_partial_rope(q, cos, sin) * (HEAD_DIM ** -0.5)).transpose(0, 2, 3, 1, 4)
    sink_b = sinks.reshape(N_KV_HEADS, GROUP).astype(jnp.float32)
    k_meta, v_meta = k[:, :, :N_META], v[:, :, :N_META]
    s_meta = jnp.einsum('bkgqd,bkmd->bkgqm', q[:, :, :, :N_META], k_meta)
    p_meta = sink_softmax(s_meta, jnp.tril(jnp.ones((N_META, N_META), dtype=bool)), sink_b[None, :, :, None, None])
    o_meta = jnp.einsum('bkgqm,bkmd->bkgqd', p_meta.astype(v.dtype), v_meta)
    q_blk = q[:, :, :, N_META:].reshape(B, N_KV_HEADS, GROUP, nB, WINDOW, HEAD_DIM)
    def band(t):
        t_meta = t[:, :, :N_META]
        t_blk = t[:, :, N_META:].reshape(B, N_KV_HEADS, nB, WINDOW, HEAD_DIM)
        prev = jnp.concatenate([jnp.zeros_like(t_blk[:, :, :1]), t_blk[:, :, :-1]], axis=2)
        meta = jnp.broadcast_to(t_meta[:, :, None], (B, N_KV_HEADS, nB, N_META, HEAD_DIM))
        return jnp.concatenate([meta, prev, t_blk], axis=3)
    k_band, v_band = band(k), band(v)
    s = jnp.einsum('bkgnqd,bknsd->bkgnqs', q_blk, k_band)
    i = jnp.arange(WINDOW)[:, None]
    j = jnp.arange(WINDOW)[None, :]
    blk = jnp.arange(nB)[:, None, None]
    mask = jnp.concatenate([
        jnp.ones((nB, WINDOW, N_META), dtype=bool),
        (j > i)[None] & (blk > 0),
        jnp.broadcast_to((j <= i)[None], (nB, WINDOW, WINDOW)),
    ], axis=-1)
    p = sink_softmax(s, mask, sink_b[None, :, :, None, None, None])
    o_real = jnp.einsum('bkgnqs,bknsd->bkgnqd', p.astype(v.dtype), v_band).reshape(B, N_KV_HEADS, GROUP, S, HEAD_DIM)
    o = jnp.concatenate([o_meta, o_real], axis=3).transpose(0, 3, 1, 2, 4).reshape(B, L, N_Q_HEADS * HEAD_DIM)
    return o @ w_out


def setup_inputs(seed: int = 0) -> dict:
    key = jax.random.key(seed)
    ks = jax.random.split(key, 16)
    nrm = lambda k, shape, fan_in: jax.random.normal(k, shape, jnp.float32) * (fan_in ** -0.5)
    return {
        "x": jax.random.normal(ks[0], (BATCH, SEQ, D_MODEL), jnp.float32),
        "meta_tokens": jax.random.normal(ks[1], (N_META, D_MODEL), jnp.float32),
        "norm_gains": 1.0 + 0.02 * jax.random.normal(ks[2], (DEPTH, 6, D_MODEL), jnp.float32),
        "w_ffn_in": nrm(ks[3], (DEPTH, 2, D_MODEL, 2 * D_FF), D_MODEL),
        "w_ffn_out": nrm(ks[4], (DEPTH, 2, D_FF, D_MODEL), D_FF),
        "gla_w_in": nrm(ks[5], (N_A_LAYERS, D_MODEL, GLA_IN), D_MODEL),
        "gla_w_gate": nrm(ks[6], (N_A_LAYERS, GLA_RANK, GLA_QK), GLA_RANK),
        "gla_b_gate": 0.1 * jax.random.normal(ks[7], (N_A_LAYERS, GLA_QK), jnp.float32),
        "gla_norm": 1.0 + 0.02 * jax.random.normal(ks[8], (N_A_LAYERS, GLA_DV), jnp.float32),
        "gla_w_out": nrm(ks[9], (N_A_LAYERS, GLA_V, D_MODEL), GLA_V),
        "kv_norm": 1.0 + 0.02 * jax.random.normal(ks[10], (D_MODEL,), jnp.float32),
        "w_kv": nrm(ks[11], (D_MODEL, 2 * N_KV_HEADS * HEAD_DIM), D_MODEL),
        "swa_w_q": nrm(ks[12], (N_B_LAYERS, D_MODEL, N_Q_HEADS * HEAD_DIM), D_MODEL),
        "swa_sinks": 0.5 * jax.random.normal(ks[13], (N_B_LAYERS, N_Q_HEADS), jnp.float32),
        "swa_w_out": nrm(ks[14], (N_B_LAYERS, N_Q_HEADS * HEAD_DIM, D_MODEL), N_Q_HEADS * HEAD_DIM),
    }


def reference(x, meta_tokens, norm_gains, w_ffn_in, w_ffn_out, gla_w_in, gla_w_gate, gla_b_gate, gla_norm,
              gla_w_out, kv_norm, w_kv, swa_w_q, swa_sinks, swa_w_out):
    B = x.shape[0]
    meta = jnp.broadcast_to(meta_tokens.astype(x.dtype)[None], (B, N_META, D_MODEL))
    h = jnp.concatenate([meta, x], axis=1)
    cos, sin = rope_tables(h.shape[1])
    k_sh, v_sh = None, None
    for layer in range(DEPTH):
        g = norm_gains[layer]
        h = h + 0.5 * rmsnorm(swiglu(rmsnorm(h, g[0]), w_ffn_in[layer, 0], w_ffn_out[layer, 0]), g[1])
        hn = rmsnorm(h, g[2])
        if layer < N_A_LAYERS:
            a = layer
            mix = gla_mixer(hn, gla_w_in[a], gla_w_gate[a], gla_b_gate[a], gla_norm[a], gla_w_out[a])
        else:
            b = layer - N_A_LAYERS
            mix = swa_mixer(hn, swa_w_q[b], swa_sinks[b], swa_w_out[b], k_sh, v_sh, cos, sin)
        h = h + rmsnorm(mix, g[3])
        h = h + 0.5 * rmsnorm(swiglu(rmsnorm(h, g[4]), w_ffn_in[layer, 1], w_ffn_out[layer, 1]), g[5])
        if layer == N_A_LAYERS - 1:
            k_sh, v_sh = shared_kv(h, kv_norm, w_kv, cos, sin)
    return h[:, N_META:]
```

```python
import numpy as np
import ml_dtypes
from contextlib import ExitStack
import concourse.bass as bass
import concourse.mybir as mybir
from concourse.bass_utils import run_bass_kernel_spmd

F32 = mybir.dt.float32
BF16 = mybir.dt.bfloat16
AF = mybir.ActivationFunctionType
ALU = mybir.AluOpType

D = 1024
DFF = 2816
NJ = 22
NMETA = 16
SEQ = 2048
G = 512
EPS = 1e-6
NEG = -30000.0
ENGS = ("pe", "act", "dve", "pool", "sp")
import os
PROF = bool(os.environ.get("KPROF"))
TAGS = {}


class Sched:
    def __init__(self, nc):
        self.nc = nc
        self.ops = {e: [] for e in ENGS}
        self.cnt = {}
        self.res_w = {}
        self.res_r = {}
        self.seen = {e: {} for e in ENGS}

    def _deps(self, eng, reads, writes, nosync_same):
        need = {}

        def add(tok):
            if tok is None:
                return
            k, v = tok
            if k == eng and nosync_same:
                return
            if need.get(k, 0) < v:
                need[k] = v

        for r in reads:
            add(self.res_w.get(r))
            if r.startswith("ps"):
                for t in self.res_r.get(r, ()):
                    if t[0] != eng:
                        add(t)
        for w in writes:
            add(self.res_w.get(w))
            for t in self.res_r.get(w, ()):
                add(t)
        waits = []
        seen = self.seen[eng]
        for k, v in need.items():
            if seen.get(k, 0) >= v:
                continue
            seen[k] = v
            waits.append((k, v))
        return waits

    def _commit(self, tok, reads, writes):
        for r in reads:
            self.res_r.setdefault(r, []).append(tok)
        for w in writes:
            self.res_w[w] = tok
            self.res_r[w] = []

    @staticmethod
    def _tag():
        import sys
        f = sys._getframe(2)
        names = []
        while f is not None and f.f_code.co_name != "build":
            names.append("%s:%d" % (f.f_code.co_name, f.f_lineno))
            f = f.f_back
        return ">".join(reversed(names))

    def op(self, eng, fn, reads=(), writes=(), nosync_same=False):
        waits = self._deps(eng, reads, writes, nosync_same)
        v = self.cnt.get(eng, 0) + 1
        self.cnt[eng] = v
        self.ops[eng].append((waits, fn, (eng, 1), self._tag() if PROF else None))
        self._commit((eng, v), reads, writes)

    def dma(self, queue, fn, semkey, reads=(), writes=()):
        waits = self._deps(queue, reads, writes, False)
        v = self.cnt.get(semkey, 0) + 16
        self.cnt[semkey] = v
        self.ops[queue].append((waits, fn, (semkey, 16), None))
        self._commit((semkey, v), reads, writes)

    def emit(self, final_wait_engine="pool"):
        nc = self.nc
        with ExitStack() as st:
            sems = {}
            for i, k in enumerate(self.cnt):
                sems[k] = st.enter_context(nc.semaphore("s%d" % i))
            fin = [(k, v) for k, v in self.cnt.items() if k not in ENGS]
            block = st.enter_context(nc.Block())

            def run(name, eng):
                for waits, fn, inc, tag in self.ops[name]:
                    for k, v in waits:
                        eng.wait_ge(sems[k], v)
                    ins = fn(eng)
                    ins.then_inc(sems[inc[0]], inc[1])
                    if tag is not None:
                        TAGS[ins.ins.name] = tag
                if name == final_wait_engine:
                    for k, v in fin:
                        eng.wait_ge(sems[k], v)

            @block.tensor
            def _(e):
                run("pe", e)

            @block.scalar
            def _(e):
                run("act", e)

            @block.vector
            def _(e):
                run("dve", e)

            @block.gpsimd
            def _(e):
                run("pool", e)

            @block.sync
            def _(e):
                run("sp", e)


def build(n_seq=2, n_grp=4):
    nc = bass.Bass("TRN2", target_bir_lowering=False)
    din = lambda n, sh, dt=F32: nc.dram_tensor(n, sh, dt, kind="ExternalInput").ap()
    dscr = lambda n, sh, dt=BF16: nc.dram_tensor(n, sh, dt, kind="Internal").ap()
    x_d = din("x", [n_seq, SEQ, D])
    meta_d = din("meta", [NMETA, D])
    gT_d = din("gT", [128, 13 * 8])
    gains_d = din("gains", [13, D])
    wi_d = din("w_ffn_in", [4, D, 2 * DFF])
    wo_d = din("w_ffn_out", [4, DFF, D])
    gwi_d = din("gla_w_in", [D, 3088])
    gwg_d = din("gla_w_gate", [16, 512])
    gb_d = din("gla_b4", [128, 4])
    gn_d = din("gla_norm", [1, 256])
    gwo_d = din("gla_w_out", [D, D])
    wkv_d = din("w_kv", [D, 512])
    wq_d = din("swa_w_q", [D, D])
    sink_d = din("swa_sinks", [1, 16])
    swo_d = din("swa_w_out", [D, D])
    ident_d = din("ident", [128, 128], BF16)
    masku_d = din("masku", [128, 128], BF16)
    mbc_d = din("mb_cur", [128, 512], BF16)
    mbp_d = din("mb_prev", [128, 512], BF16)
    rst_d = din("rstmask", [128, 512], BF16)
    cs_d = din("cs", [NMETA + SEQ, 16])
    y_d = nc.dram_tensor("y", [n_seq, SEQ, D], F32, kind="ExternalOutput").ap()

    wi_b = dscr("wi_b", [4, D, 2 * DFF])
    wo_b = dscr("wo_b", [4, DFF, D])
    gwi_b = dscr("gwi_b", [D, 3088])
    gwg_b = dscr("gwg_b", [16, 512])
    gwo_b = dscr("gwo_b", [D, D])
    wkv_b = dscr("wkv_b", [D, 512])
    wq_b = dscr("wq_b", [D, D])
    swo_b = dscr("swo_b", [D, D])

    s = Sched(nc)
    st = ExitStack()
    sb = lambda n, sh, dt: st.enter_context(nc.sbuf_tensor(n, sh, dt))
    h = sb("h", [128, 5, D], F32)
    xs = sb("xs", [128, 3, D], BF16)
    XnT = sb("XnT", [128, 8, G], BF16)
    XnTk = sb("XnTk", [128, 8, G], BF16)
    XnTm = sb("XnTm", [128, 8, 16], BF16)
    XnTkm = sb("XnTkm", [128, 8, 16], BF16)
    BIG = sb("BIG", [128, 12288], BF16)
    wblk = sb("wblk", [128, 4, 8, 512], BF16)
    wout = sb("wout", [128, NJ, D], BF16)
    gB = sb("gB", [128, 2, D], F32)
    tmp = sb("tmp", [128, 4, 512], F32)
    S = sb("S", [128, 4, 256], F32)
    Sb = sb("Sb", [128, 4, 256], BF16)
    Sm = sb("Sm", [128, 4, 256], F32)
    Smb = sb("Smb", [128, 4, 256], BF16)
    kT2 = sb("kT2", [128, 4, 656], BF16)
    vaug = sb("vaug", [128, 6, 4, 65], BF16)
    ident = sb("identS", [128, 128], BF16)
    masku = sb("maskuS", [128, 128], BF16)
    mbc = sb("mbcS", [128, 512], BF16)
    mbp = sb("mbpS", [128, 512], BF16)
    rst = sb("rstS", [128, 512], BF16)
    gT = sb("gTS", [128, 13 * 8], F32)
    negb = sb("negb", [128, 4], F32)
    wgate = sb("wgate", [16, 512], BF16)
    wlr = sb("wlr", [128, 8, 16], BF16)
    cs = sb("csS", [128, 4, 16], F32)
    csm = sb("csm", [16, 16], F32)
    esink = sb("esink", [128, 16], F32)
    ghead = sb("ghead", [128, 256], F32)
    small = sb("small", [128, 64], F32)
    dec = sb("dec", [128, 4, 4], F32)
    nlast = sb("nlast", [128, 4, 4], F32)
    attsb = sb("attsb", [128, 2, 128], BF16)
    kf = sb("kf", [128, 256], F32)
    kdup = sb("kdup", [128, 4, 4, 2, 64], BF16)
    rt = sb("rt", [128, 4, 128], F32)
    sg = sb("sg", [128, 2, 512], F32)
    junk = sg[:, 0, :].bitcast(BF16)
    den = sb("den", [128, 2, 4], F32)
    srb2 = sb("srb2", [128, 2, D], BF16)
    ob2 = sb("ob2", [128, 2, D], BF16)
    yTb = sb("yTb", [128, 8, 128], BF16)
    ps = st.enter_context(nc.psum_tensor("ps", [128, 8, 512], F32))

    GT = BIG[:, 0:NJ * G].rearrange("p (j t) -> p j t", j=NJ)
    qd = BIG[:, 0:2048].rearrange("p (h t) -> p h t", h=4)
    kd = BIG[:, 2048:4096].rearrange("p (h t) -> p h t", h=4)
    ke = BIG[:, 4096:6144].rearrange("p (h t) -> p h t", h=4)
    vtok = BIG[:, 6144:10240].rearrange("p (t c) -> p t c", t=4)
    kendt = BIG[:, 11264:11776]
    lrT = BIG[:, 11776:12288]
    qT = BIG[:, 0:4096].rearrange("p (k t) -> p k t", k=8)
    pT = BIG[:, 4096:4096 + 6 * 512].rearrange("p (s t) -> p s t", s=6)
    tmpA = tmp[:, 0:2, :].rearrange("p a b -> p (a b)")
    tmpB = tmp[:, 2:4, :].rearrange("p a b -> p (a b)")
    RA = ["tq0", "tq1"]
    RB = ["tq2", "tq3"]
    rBIG = lambda a, b: ["big%d" % i for i in range(a // 512, (b + 511) // 512)]
    P0, P1 = 4, 6
    epsT = small[:, 63:64]

    def psbf(b):
        return ps[:, b, :].bitcast(BF16)

    state = {"bankA": 0, "pairB": 0, "wslot": 0, "xs": 0, "sm": 0, "gb": 0}

    def bankA():
        b = state["bankA"]
        state["bankA"] = (b + 1) % 4
        return b

    def pairB():
        b = state["pairB"]
        state["pairB"] = 1 - b
        return 4 + 2 * b

    def wslot():
        b = state["wslot"]
        state["wslot"] = (b + 1) % 4
        return b

    def xslot():
        b = state["xs"]
        state["xs"] = 1 - b
        return b

    def smcol():
        b = state["sm"]
        state["sm"] = 1 - b
        return 24 * b

    def cload(dst, src, key):
        s.dma("pool", lambda e: e.dma_start(out=dst, in_=src), key, writes=[key])

    cload(ident[:], ident_d, "ident")
    cload(gT[:], gT_d, "gT")
    cload(masku[:], masku_d, "masku")
    cload(mbc[:], mbc_d, "mbc")
    cload(mbp[:], mbp_d, "mbp")
    cload(rst[:], rst_d, "rst")
    cload(negb[:], gb_d, "negb_raw")
    cload(esink[:], sink_d.partition_broadcast(128), "esink_raw")
    cload(ghead[:], gn_d.partition_broadcast(128), "ghead")
    cload(csm[:], cs_d[0:16, :], "csm")
    s.op("dve", lambda e: e.tensor_scalar(out=negb[:], in0=negb[:], scalar1=-1.0, scalar2=None, op0=ALU.mult),
         reads=["negb_raw"], writes=["negb"])
    s.op("act", lambda e: e.activation(out=esink[:], in_=esink[:], func=AF.Exp), reads=["esink_raw"], writes=["esink"])
    s.op("pool", lambda e: e.memset(small[:, 62:63], float(np.log(0.5))), writes=["lnhalf"])
    s.op("pool", lambda e: e.memset(small[:, 63:64], EPS), reads=["lnhalf"], writes=["eps"])
    s.op("pool", lambda e: e.memset(vaug[:], 1.0), writes=["vaug_init"])
    s.op("pool", lambda e: e.memset(S[:], 0.0), writes=["S0", "S1", "S2", "S3"])
    s.op("pool", lambda e: e.memset(Sb[:], 0.0), writes=["Sb0", "Sb1", "Sb2", "Sb3"])

    cast_hist = []

    def cast(dst, src, key):
        after = cast_hist[-3:-2]
        s.dma("pool", lambda e: e.dma_start(out=dst, in_=src), key, reads=after, writes=[key])
        cast_hist.append(key)

    SB4 = DFF // 2

    def wi_keys(m, c0, ncol):
        return sorted({"c_wi%d_%d" % (m, c // SB4) for c in (c0, c0 + ncol - 1)})

    def cast_ffn(m):
        for q in (0, 2, 1, 3):
            cast(wi_b[m, :, q * SB4:(q + 1) * SB4], wi_d[m, :, q * SB4:(q + 1) * SB4], "c_wi%d_%d" % (m, q))
        cast(wo_b[m], wo_d[m], "c_wo%d" % m)

    def cast_gla():
        cast(gwg_b, gwg_d, "c_gwg")
        cast(gwi_b[:, 3072:3088], gwi_d[:, 3072:3088], "c_gwi6")
        for q in range(6):
            cast(gwi_b[:, q * 512:(q + 1) * 512], gwi_d[:, q * 512:(q + 1) * 512], "c_gwi%d" % q)
        cast(gwo_b, gwo_d, "c_gwo")

    def load_gla_consts():
        s.dma("sp", lambda e: e.dma_start(out=wgate[:], in_=gwg_b), "wgate", reads=["c_gwg"], writes=["wgate"])
        s.dma("sp", lambda e: e.dma_start(out=wlr[:], in_=gwi_b.rearrange("(kc p) c -> p kc c", p=128)[:, :, 3072:3088]),
              "wlr", reads=["c_gwi6"], writes=["wlr"])

    def cast_kv():
        cast(wkv_b, wkv_d, "c_wkv")

    def cast_swa():
        cast(wq_b, wq_d, "c_wq")
        cast(swo_b, swo_d, "c_swo")

    def load_blk(src2d, c0, ncol, casts):
        sl = wslot()
        key = "wblk%d" % sl
        s.dma("sp", lambda e: e.dma_start(out=wblk[:, sl, :, 0:ncol],
                                          in_=src2d.rearrange("(kc p) c -> p kc c", p=128)[:, :, c0:c0 + ncol]),
              key, reads=casts, writes=[key])
        return sl, key

    def load_wout(src2d, nchunk, casts):
        v = src2d.rearrange("(j p) c -> p j c", p=128)
        for p0 in range(0, nchunk, 2):
            key = "wout%d" % (p0 // 2)
            s.dma("sp", lambda e, p0=p0: e.dma_start(out=wout[:, p0:p0 + 2, :], in_=v[:, p0:p0 + 2, :]),
                  key, reads=casts, writes=[key])

    def load_gB(gi):
        gs_ = 1 - state["gb"]
        state["gb"] = gs_
        s.dma("sp", lambda e: e.dma_start(out=gB[:, gs_, :], in_=gains_d[gi:gi + 1, :].partition_broadcast(128)),
              "gB%d" % gs_, writes=["gB%d" % gs_])

    def rstd_from_ss(col, T, n, scale, half=False):
        s.op("act", lambda e: e.activation(out=small[:T, col + 8:col + 8 + n], in_=small[:T, col:col + n], func=AF.Ln,
                                           scale=scale, bias=epsT[:T, :]),
             reads=["ss%d" % col, "eps"], writes=["ln%d" % col])
        if half:
            s.op("act", lambda e: e.activation(out=small[:T, col + 16:col + 16 + n], in_=small[:T, col + 8:col + 8 + n],
                                               func=AF.Exp, scale=-0.5, bias=small[:T, 62:63]),
                 reads=["ln%d" % col, "eps"], writes=["rstd%d" % col])
        else:
            s.op("act", lambda e: e.activation(out=small[:T, col + 16:col + 16 + n], in_=small[:T, col + 8:col + 8 + n],
                                               func=AF.Exp, scale=-0.5),
                 reads=["ln%d" % col], writes=["rstd%d" % col])

    def pre_A(T, ht):
        c = smcol()
        s.op("act", lambda e: e.activation(out=junk[:T, :], in_=h[:T, ht, :], func=AF.Square, accum_out=small[:T, c:c + 1]),
             reads=["h%d" % ht], writes=["ss%d" % c, "junk", "sg0"])
        rstd_from_ss(c, T, 1, 1.0 / D)
        xi = 2 if ht == 4 else xslot()
        s.op("act", lambda e: e.activation(out=xs[:T, xi, :], in_=h[:T, ht, :], func=AF.Copy, scale=small[:T, c + 16:c + 17]),
             reads=["h%d" % ht, "rstd%d" % c], writes=["xs%d" % xi])
        return xi

    def transpose8(T, src_fn, src_res):
        b = bankA()
        pv = psbf(b)[:, 0:8 * T].rearrange("p (k t) -> p k t", k=8)
        for kc in range(8):
            s.op("pe", lambda e, kc=kc: e.transpose(out=pv[:, kc, :], in_=src_fn(kc), identity=ident[:T, :T]),
                 reads=src_res + ["ident"], writes=["ps%d" % b], nosync_same=True)
        return b, pv

    def pre_B(T, t, xi, outs):
        b, pv = transpose8(T, lambda kc: xs[:T, xi, kc * 128:(kc + 1) * 128], ["xs%d" % xi])
        for buf, rp, gi in outs:
            s.op("dve", lambda e, buf=buf, gi=gi: e.tensor_tensor(
                out=buf[:, :, t * T:(t + 1) * T], in0=pv,
                in1=gT[:, gi * 8:(gi + 1) * 8].unsqueeze(2).to_broadcast([128, 8, T]), op=ALU.mult),
                reads=["ps%d" % b, "gT"], writes=["%s%d" % (rp, t)])

    def own_prenorm(T, NT, h0, outs):
        pend = None
        for t in range(NT):
            xi = pre_A(T, h0 + t)
            if pend is not None:
                pre_B(T, *pend, outs)
            pend = (t, xi)
        pre_B(T, *pend, outs)

    class Nxt:
        def __init__(self, outs, before=None):
            self.outs = outs
            self.before = before

        def A(self, t):
            if self.before is not None:
                self.before(t)
            return pre_A(128, t)

        def B(self, t, xi):
            pre_B(128, t, xi, self.outs)

    def pipeline(NT, stA, stN, stP, nxt):
        pend = {}
        stA(0)
        stN(0)
        for t in range(1, NT):
            stA(t)
            if nxt is not None and t >= 2:
                nxt.B(t - 2, pend.pop(t - 2))
            stP(t - 1)
            stN(t)
            if nxt is not None:
                pend[t - 1] = nxt.A(t - 1)
        stP(NT - 1)
        if nxt is not None and NT >= 2:
            nxt.B(NT - 2, pend.pop(NT - 2))
        if nxt is not None:
            xi = nxt.A(NT - 1)
            return lambda: nxt.B(NT - 1, xi)
        return None

    def mm_tokmajor(T, lhs_fn, lhs_res, blocks, pb):
        for half in range(2):
            sl, key = blocks[half]
            for kc in range(8):
                s.op("pe", lambda e, kc=kc, half=half, sl=sl: e.matmul(
                    ps[:T, pb + half, :], lhsT=lhs_fn(kc), rhs=wblk[:, sl, kc, :], start=(kc == 0), stop=(kc == 7)),
                    reads=lhs_res + [key], writes=["ps%d" % (pb + half)], nosync_same=True)

    def mm_wout(T, lhs_fn, lhs_res, nchunk, pb, mid=None):
        for half in range(2):
            if half == 1 and mid is not None:
                mid()
            for j in range(nchunk):
                s.op("pe", lambda e, j=j, half=half: e.matmul(
                    ps[:T, pb + half, :], lhsT=lhs_fn(j), rhs=wout[:, j, half * 512:(half + 1) * 512],
                    start=(j == 0), stop=(j == nchunk - 1)),
                    reads=lhs_res + ["wout%d" % (j // 2)], writes=["ps%d" % (pb + half)], nosync_same=True)

    def postnorm(ht, T, pb, half):
        c = smcol()
        src = ps[:T, pb:pb + 2, :].rearrange("p a b -> p (a b)")
        s.op("act", lambda e: e.activation(out=junk[:T, :], in_=src, func=AF.Square, accum_out=small[:T, c:c + 1]),
             reads=["ps%d" % pb, "ps%d" % (pb + 1)], writes=["ss%d" % c, "junk", "sg0"])
        gs_ = state["gb"]
        s.op("dve", lambda e: e.tensor_tensor(out=tmpA[:T, :], in0=src, in1=gB[:T, gs_, :], op=ALU.mult),
             reads=["ps%d" % pb, "ps%d" % (pb + 1), "gB%d" % gs_], writes=RA)
        rstd_from_ss(c, T, 1, 1.0 / D, half=half)
        s.op("dve", lambda e: e.scalar_tensor_tensor(out=h[:T, ht, :], in0=tmpA[:T, :], scalar=small[:T, c + 16:c + 17], in1=h[:T, ht, :],
                                                     op0=ALU.mult, op1=ALU.add),
             reads=RA + ["rstd%d" % c, "h%d" % ht], writes=["h%d" % ht])

    def ffn(m, gi_pre, gi_post, T, NT, h0, own_pre, nxt, XnT=XnT, xp="XnT", pre=None):
        if pre is not None and "F" in os.environ.get("KDBG", ""):
            pre()
            pre = None
        GW = T * NT
        if own_pre:
            own_prenorm(T, NT, h0, [(XnT, xp, gi_pre)])
        xres = [xp + "%d" % t for t in range(NT)]
        nblk = 6
        pend = []

        def issue(bi):
            c0 = bi * 512
            ncol = min(512, DFF - c0)
            g_ = load_blk(wi_b[m], c0, ncol, wi_keys(m, c0, ncol))
            u_ = load_blk(wi_b[m], DFF + c0, ncol, wi_keys(m, DFF + c0, ncol))
            pend.append((g_, u_, ncol))

        issue(0)
        issue(1)
        load_gB(gi_post)
        for bi in range(nblk):
            (gs, gk), (us, uk), ncol = pend[bi]
            if bi == 0:
                load_wout(wo_b[m], NJ, ["c_wo%d" % m])
            def chunk(jj, c0, c1, bi=bi, gs=gs, gk=gk, us=us, uk=uk):
                j = bi * 4 + jj
                bg, bu = (0, 1) if j % 2 == 0 else (2, 3)
                xr = [xp + "%d" % t for t in range(c0 // T, (c1 + T - 1) // T)]
                for kc in range(8):
                    s.op("pe", lambda e, kc=kc: e.matmul(
                        ps[:, bg, c0:c1], lhsT=wblk[:, gs, kc, jj * 128:(jj + 1) * 128], rhs=XnT[:, kc, c0:c1],
                        start=(kc == 0), stop=(kc == 7)),
                        reads=xr + [gk], writes=["ps%d" % bg], nosync_same=True)
                for kc in range(8):
                    s.op("pe", lambda e, kc=kc: e.matmul(
                        ps[:, bu, c0:c1], lhsT=wblk[:, us, kc, jj * 128:(jj + 1) * 128], rhs=XnT[:, kc, c0:c1],
                        start=(kc == 0), stop=(kc == 7)),
                        reads=xr + [uk], writes=["ps%d" % bu], nosync_same=True)
                si = j % 2
                s.op("act", lambda e: e.activation(out=sg[:, si, c0:c1], in_=ps[:, bg, c0:c1], func=AF.Silu),
                     reads=["ps%d" % bg], writes=["sg%d" % si])
                s.op("dve", lambda e: e.tensor_tensor(out=GT[:, j, c0:c1], in0=sg[:, si, c0:c1], in1=ps[:, bu, c0:c1], op=ALU.mult),
                     reads=["sg%d" % si, "ps%d" % bu], writes=rBIG(j * 512, j * 512 + 512))

            if bi == 0 and pre is not None and NT == 4:
                for jj in range(4):
                    chunk(jj, 0, 3 * T)
                pre()
                for jj in range(4):
                    chunk(jj, 3 * T, 4 * T)
            else:
                if bi == 0 and pre is not None:
                    pre()
                for jj in range(ncol // 128):
                    chunk(jj, 0, GW)
            if bi + 2 < nblk:
                issue(bi + 2)
        gtres = rBIG(0, NJ * 512)
        pendq = []
        for t in range(NT):
            pb = pairB()
            mid = None
            if len(pendq) == 2:
                p_ = pendq.pop(0)
                mid = lambda p_=p_: nxt.B(*p_)
            mm_wout(T, lambda j, t=t: GT[:, j, t * T:(t + 1) * T], gtres, NJ, pb, mid)
            postnorm(h0 + t, T, pb, True)
            if nxt is not None:
                pendq.append((t, nxt.A(t)))
        if pendq:
            while len(pendq) > 1:
                nxt.B(*pendq.pop(0))
            last = pendq[0]
            return lambda: nxt.B(*last)
        return None

    def gla(T, NT, h0, own_pre, nxt, XnT=XnT, xp="XnT", pre=None):
        if pre is not None and "G" in os.environ.get("KDBG", ""):
            pre()
            pre = None
        GW = T * NT
        if own_pre:
            own_prenorm(T, NT, h0, [(XnT, xp, 2)])
        load_gB(3)
        xres = [xp + "%d" % t for t in range(NT)]
        bv0 = load_blk(gwi_b, 1024, 512, ["c_gwi2"])
        bv1 = load_blk(gwi_b, 1536, 512, ["c_gwi3"])
        bq = load_blk(gwi_b, 0, 512, ["c_gwi0"])
        bk = load_blk(gwi_b, 512, 512, ["c_gwi1"])
        load_wout(gwo_b, 8, ["c_gwo"])

        def vproj(t):
            pb = pairB()
            mm_tokmajor(T, lambda kc: XnT[:, kc, t * T:(t + 1) * T], [xp + "%d" % t], [bv0, bv1], pb)
            s.op("act", lambda e: e.activation(out=vtok[:T, t, :], in_=ps[:T, pb:pb + 2, :].rearrange("p a b -> p (a b)"),
                                               func=AF.Copy),
                 reads=["ps%d" % pb, "ps%d" % (pb + 1)], writes=rBIG(6144 + t * 1024, 6144 + t * 1024 + 1024))

        for t_ in range(NT - 1):
            vproj(t_)
        if pre is not None:
            pre()
        vproj(NT - 1)
        b = bankA()
        for kc in range(8):
            s.op("pe", lambda e, kc=kc, b=b: e.matmul(ps[:16, b, 0:GW], lhsT=wlr[:, kc, :], rhs=XnT[:, kc, 0:GW],
                                                      start=(kc == 0), stop=(kc == 7)),
                 reads=xres + ["wlr"], writes=["ps%d" % b], nosync_same=True)
        s.op("act", lambda e, b=b: e.activation(out=lrT[:16, 0:GW], in_=ps[:16, b, 0:GW], func=AF.Copy),
             reads=["ps%d" % b], writes=rBIG(11776, 12288))
        Lr, Eq, Ek, Ee = tmp[:, 0, :], tmp[:, 1, :], tmp[:, 2, :], tmp[:, 3, :]

        def gate_head(hd):
            b = bankA()
            s.op("pe", lambda e: e.matmul(ps[:, b, 0:GW], lhsT=wgate[:, hd * 128:(hd + 1) * 128], rhs=lrT[:16, 0:GW],
                                          start=True, stop=True),
                 reads=rBIG(11776, 12288) + ["wgate"], writes=["ps%d" % b], nosync_same=True)
            b1 = bankA()
            for kc in range(8):
                s.op("pe", lambda e, kc=kc: e.matmul(ps[:, b1, 0:GW], lhsT=wblk[:, bq[0], kc, hd * 128:(hd + 1) * 128],
                                                     rhs=XnT[:, kc, 0:GW], start=(kc == 0), stop=(kc == 7)),
                     reads=xres + [bq[1]], writes=["ps%d" % b1], nosync_same=True)
            b2 = bankA()
            for kc in range(8):
                s.op("pe", lambda e, kc=kc: e.matmul(ps[:, b2, 0:GW], lhsT=wblk[:, bk[0], kc, hd * 128:(hd + 1) * 128],
                                                     rhs=XnT[:, kc, 0:GW], start=(kc == 0), stop=(kc == 7)),
                     reads=xres + [bk[1]], writes=["ps%d" % b2], nosync_same=True)
            s.op("act", lambda e: e.activation(out=Eq[:, 0:GW], in_=ps[:, b, 0:GW], func=AF.Exp, scale=-1.0,
                                               bias=negb[:, hd:hd + 1]),
                 reads=["ps%d" % b, "negb"], writes=["tq1"])
            s.op("act", lambda e: e.activation(out=Ek[:, 0:GW], in_=Eq[:, 0:GW], func=AF.Ln, bias=1.0),
                 reads=["tq1"], writes=["tq2"])
            s.op("dve", lambda e: e.tensor_tensor_scan(out=Lr[:, 0:GW], data0=rst[:, 0:GW], data1=Ek[:, 0:GW], initial=0.0,
                                                       op0=ALU.mult, op1=ALU.add),
                 reads=["tq2", "rst"], writes=["tq0"])
            s.op("act", lambda e: e.activation(out=Eq[:, 0:GW], in_=Lr[:, 0:GW], func=AF.Exp, scale=-1.0 / 16),
                 reads=["tq0"], writes=["tq1"])
            s.op("act", lambda e: e.activation(out=Ek[:, 0:GW], in_=Lr[:, 0:GW], func=AF.Exp, scale=1.0 / 16),
                 reads=["tq0"], writes=["tq2"])
            for t in range(NT):
                s.op("dve", lambda e, t=t: e.tensor_scalar(out=nlast[:, hd, t:t + 1], in0=Lr[:, (t + 1) * T - 1:(t + 1) * T],
                                                           scalar1=-1.0 / 16, scalar2=None, op0=ALU.mult),
                     reads=["tq0"], writes=["nlast"])
                s.op("act", lambda e, t=t: e.activation(out=Ee[:, t * T:(t + 1) * T], in_=Lr[:, t * T:(t + 1) * T],
                                                        func=AF.Exp, scale=1.0 / 16, bias=nlast[:, hd, t:t + 1]),
                     reads=["tq0", "nlast"], writes=["tq3"])
                s.op("act", lambda e, t=t: e.activation(out=dec[:, hd, t:t + 1], in_=nlast[:, hd, t:t + 1], func=AF.Exp),
                     reads=["nlast"], writes=["dec"])
            s.op("dve", lambda e: e.scalar_tensor_tensor(out=qd[:, hd, 0:GW], in0=ps[:, b1, 0:GW], scalar=128.0 ** -0.5,
                                                         in1=Eq[:, 0:GW], op0=ALU.mult, op1=ALU.mult),
                 reads=["ps%d" % b1, "tq1"], writes=rBIG(hd * 512, hd * 512 + 512))
            s.op("dve", lambda e: e.tensor_tensor(out=kd[:, hd, 0:GW], in0=ps[:, b2, 0:GW], in1=Ek[:, 0:GW], op=ALU.mult),
                 reads=["ps%d" % b2, "tq2"], writes=rBIG(2048 + hd * 512, 2048 + hd * 512 + 512))
            s.op("dve", lambda e: e.tensor_tensor(out=ke[:, hd, 0:GW], in0=ps[:, b2, 0:GW], in1=Ee[:, 0:GW], op=ALU.mult),
                 reads=["ps%d" % b2, "tq3"], writes=rBIG(4096 + hd * 512, 4096 + hd * 512 + 512))

        for hd_ in range(4):
            gate_head(hd_)
        br0 = load_blk(gwi_b, 2048, 512, ["c_gwi4"])
        br1 = load_blk(gwi_b, 2560, 512, ["c_gwi5"])

        def stA(t):
            tc0, tc1 = t * T, (t + 1) * T
            vres = rBIG(6144 + t * 1024, 6144 + t * 1024 + 1024)
            mm_tokmajor(T, lambda kc: XnT[:, kc, tc0:tc1], [xp + "%d" % t], [br0, br1], P0)
            s.op("act", lambda e: e.activation(out=srb2[:T, t % 2, :], in_=ps[:T, P0:P0 + 2, :].rearrange("p a b -> p (a b)"),
                                               func=AF.Silu),
                 reads=["ps%d" % P0, "ps%d" % (P0 + 1)], writes=["srb%d" % (t % 2)])

            def headA(hd):
                qres = rBIG(hd * 512, hd * 512 + 512)
                kres = rBIG(2048 + hd * 512, 2048 + hd * 512 + 512)
                keres = rBIG(4096 + hd * 512, 4096 + hd * 512 + 512)
                b = bankA()
                s.op("pe", lambda e: e.matmul(ps[:T, b, 0:T], lhsT=kd[:, hd, tc0:tc1], rhs=qd[:, hd, tc0:tc1], start=True, stop=True),
                     reads=qres + kres, writes=["ps%d" % b], nosync_same=True)
                b2 = bankA()
                s.op("pe", lambda e: e.transpose(out=psbf(b2)[:T, 0:128], in_=ke[:, hd, tc0:tc1], identity=ident[:, :]),
                     reads=keres + ["ident"], writes=["ps%d" % b2], nosync_same=True)
                ai = hd % 2
                s.op("dve", lambda e: e.tensor_tensor(out=attsb[:T, ai, 0:T], in0=ps[:T, b, 0:T], in1=masku[:T, 0:T], op=ALU.mult),
                     reads=["ps%d" % b, "masku"], writes=["att%d" % ai])
                s.op("act", lambda e: e.activation(out=kendt[:T, hd * 128:(hd + 1) * 128], in_=psbf(b2)[:T, 0:128], func=AF.Copy),
                     reads=["ps%d" % b2], writes=["kendt%d" % hd])

            def headB(hd):
                qres = rBIG(hd * 512, hd * 512 + 512)
                ai = hd % 2
                ob, oc = P1 + hd // 2, (hd % 2) * 256
                s.op("pe", lambda e: e.matmul(ps[:T, ob, oc:oc + 256], lhsT=qd[:, hd, tc0:tc1], rhs=Sb[:, hd, :], start=True, stop=False),
                     reads=qres + ["Sb%d" % hd], writes=["ps%d" % ob], nosync_same=True)
                s.op("pe", lambda e: e.matmul(ps[:T, ob, oc:oc + 256], lhsT=attsb[:T, ai, 0:T],
                                              rhs=vtok[:T, t, hd * 256:(hd + 1) * 256], start=False, stop=True),
                     reads=["att%d" % ai] + vres, writes=["ps%d" % ob], nosync_same=True)
                b3 = bankA()
                s.op("pe", lambda e: e.matmul(ps[:, b3, 0:256], lhsT=kendt[:T, hd * 128:(hd + 1) * 128],
                                              rhs=vtok[:T, t, hd * 256:(hd + 1) * 256], start=True, stop=True),
                     reads=["kendt%d" % hd] + vres, writes=["ps%d" % b3], nosync_same=True)
                s.op("dve", lambda e: e.scalar_tensor_tensor(out=S[:, hd, :], in0=S[:, hd, :], scalar=dec[:, hd, t:t + 1],
                                                             in1=ps[:, b3, 0:256], op0=ALU.mult, op1=ALU.add),
                     reads=["S%d" % hd, "dec", "ps%d" % b3], writes=["S%d" % hd])
                s.op("act", lambda e: e.activation(out=Sb[:, hd, :], in_=S[:, hd, :], func=AF.Copy),
                     reads=["S%d" % hd], writes=["Sb%d" % hd])

            headA(0)
            for hd_ in range(1, 4):
                headA(hd_)
                headB(hd_ - 1)
            headB(3)

        def stN(t):
            for hd in range(4):
                ob, oc = P1 + hd // 2, (hd % 2) * 256
                s.op("act", lambda e, hd=hd, ob=ob, oc=oc: e.activation(out=junk[:T, 0:256], in_=ps[:T, ob, oc:oc + 256], func=AF.Square,
                                                                        accum_out=small[:T, 44 + hd:45 + hd]),
                     reads=["ps%d" % ob], writes=["ssh%d" % hd, "junk", "sg0"])
            s.op("act", lambda e: e.activation(out=small[:T, 48:52], in_=small[:T, 44:48], func=AF.Ln, scale=1.0 / 256, bias=epsT[:T, :]),
                 reads=["ssh%d" % i for i in range(4)] + ["eps"], writes=["lnh"])
            s.op("act", lambda e: e.activation(out=small[:T, 52:56], in_=small[:T, 48:52], func=AF.Exp, scale=-0.5),
                 reads=["lnh"], writes=["rstdh"])
            for hd in range(4):
                ob, oc = P1 + hd // 2, (hd % 2) * 256
                s.op("act", lambda e, hd=hd, ob=ob, oc=oc: e.activation(out=tmpB[:T, hd * 256:(hd + 1) * 256], in_=ps[:T, ob, oc:oc + 256],
                                                                        func=AF.Copy, scale=small[:T, 52 + hd:53 + hd]),
                     reads=["ps%d" % ob, "rstdh"], writes=RB)
            s.op("dve", lambda e: e.tensor_tensor(out=tmpB[:T, :].rearrange("p (h c) -> p h c", h=4),
                                                  in0=tmpB[:T, :].rearrange("p (h c) -> p h c", h=4),
                                                  in1=ghead[:T, :].unsqueeze(1).to_broadcast([T, 4, 256]), op=ALU.mult),
                 reads=RB + ["ghead"], writes=RB)
            s.op("dve", lambda e: e.tensor_tensor(out=ob2[:T, t % 2, :], in0=tmpB[:T, :], in1=srb2[:T, t % 2, :], op=ALU.mult),
                 reads=RB + ["srb%d" % (t % 2)], writes=["ob%d" % (t % 2)])

        def stP(t):
            b, pv = transpose8(T, lambda kc: ob2[:T, t % 2, kc * 128:(kc + 1) * 128], ["ob%d" % (t % 2)])
            s.op("act", lambda e: e.activation(out=yTb[:, :, 0:T], in_=pv, func=AF.Copy), reads=["ps%d" % b], writes=["yTb"])
            mm_wout(T, lambda j: yTb[:, j, 0:T], ["yTb"], 8, P0)
            postnorm(h0 + t, T, P0, False)

        return pipeline(NT, stA, stN, stP, nxt)

    def rope(src3, res, T, nh, csT):
        x1, x2 = src3[:, :, 0:8], src3[:, :, 8:16]
        cosB = csT[:, 0:8].unsqueeze(1).to_broadcast([T, nh, 8])
        sinB = csT[:, 8:16].unsqueeze(1).to_broadcast([T, nh, 8])
        r = [rt[:T, i, 0:nh * 8].rearrange("p (h c) -> p h c", h=nh) for i in range(4)]
        s.op("dve", lambda e: e.tensor_tensor(out=r[0], in0=x1, in1=cosB, op=ALU.mult), reads=res + ["cs", "csm"], writes=["rt0"])
        s.op("dve", lambda e: e.tensor_tensor(out=r[1], in0=x2, in1=sinB, op=ALU.mult), reads=res + ["cs"], writes=["rt1"])
        s.op("dve", lambda e: e.tensor_tensor(out=r[2], in0=x2, in1=cosB, op=ALU.mult), reads=res + ["cs"], writes=["rt2"])
        s.op("dve", lambda e: e.tensor_tensor(out=r[3], in0=x1, in1=sinB, op=ALU.mult), reads=res + ["cs"], writes=["rt3"])
        s.op("dve", lambda e: e.tensor_tensor(out=x1, in0=r[0], in1=r[1], op=ALU.subtract), reads=["rt0", "rt1"], writes=res)
        s.op("dve", lambda e: e.tensor_tensor(out=x2, in0=r[2], in1=r[3], op=ALU.add), reads=["rt2", "rt3"], writes=res)

    def kv(T, NT, h0, slot0, col0, own_pre, csbuf, XnTk=XnTk, xp="XnTk", pre=None, defer=False):
        if pre is not None and "K" in os.environ.get("KDBG", ""):
            pre()
            pre = None
        if own_pre:
            own_prenorm(T, NT, h0, [(XnTk, xp, 12)])
        bw = load_blk(wkv_b, 0, 512, ["c_wkv"])

        def K1(t):
            b = bankA()
            for kc in range(8):
                s.op("pe", lambda e, kc=kc: e.matmul(ps[:T, b, :], lhsT=XnTk[:, kc, t * T:(t + 1) * T], rhs=wblk[:, bw[0], kc, :],
                                                     start=(kc == 0), stop=(kc == 7)),
                     reads=[xp + "%d" % t, bw[1]], writes=["ps%d" % b], nosync_same=True)
            s.op("act", lambda e: e.activation(out=vaug[:T, slot0 + t, :, 0:64],
                                               in_=ps[:T, b, 256:512].rearrange("p (h c) -> p h c", h=4), func=AF.Copy),
                 reads=["ps%d" % b, "vaug_init"], writes=["vaug%d" % (slot0 + t)])
            s.op("act", lambda e: e.activation(out=kf[:T, :], in_=ps[:T, b, 0:256], func=AF.Copy),
                 reads=["ps%d" % b], writes=["kf"])
            rope(kf[:T, :].rearrange("p (h c) -> p h c", h=4), ["kf"], T, 4, csbuf(t))
            for dd in range(2):
                s.op("pool", lambda e, dd=dd: e.tensor_copy(out=kdup[:T, t % 4, :, dd, :], in_=kf[:T, :].rearrange("p (h c) -> p h c", h=4)),
                     reads=["kf"], writes=["kdup%d" % (t % 4)])

        def K2(t):
            b2 = bankA()
            pv = psbf(b2)[:, 0:4 * T].rearrange("p (k t) -> p k t", k=4)
            for hd in range(4):
                s.op("pe", lambda e, hd=hd: e.transpose(out=pv[:, hd, :], in_=kdup[:T, t % 4, hd, :, :].rearrange("p a b -> p (a b)"),
                                                        identity=ident[:T, :T]),
                     reads=["kdup%d" % (t % 4), "ident"], writes=["ps%d" % b2], nosync_same=True)
            s.op("act", lambda e: e.activation(out=kT2[:, :, col0 + t * T:col0 + (t + 1) * T], in_=pv, func=AF.Copy),
                 reads=["ps%d" % b2], writes=["kT2_%d" % (slot0 + t)])

        for t_ in range(NT - 1):
            K1(t_)

        def tail():
            if pre is not None:
                pre()
            K1(NT - 1)
            for t_ in range(NT):
                K2(t_)

        if defer:
            return tail
        tail()
        return None

    def swa(first_grp, own_pre, nxt, pre=None):
        if pre is not None and "S" in os.environ.get("KDBG", ""):
            pre()
            pre = None
        T, NT = 128, 4
        if own_pre:
            own_prenorm(T, NT, 0, [(XnT, "XnT", 8)])
        load_gB(9)
        bq0 = load_blk(wq_b, 0, 512, ["c_wq"])
        bq1 = load_blk(wq_b, 512, 512, ["c_wq"])
        load_wout(swo_b, 8, ["c_swo"])

        qbuf = [(ob2[:, 0, :], "ob0"), (ob2[:, 1, :], "ob1"), (yTb[:, :, :].rearrange("p a b -> p (a b)"), "yTb")]

        def Q1(t):
            pq = pairB()
            mm_tokmajor(T, lambda kc: XnT[:, kc, t * T:(t + 1) * T], ["XnT%d" % t], [bq0, bq1], pq)
            s.op("act", lambda e: e.activation(out=tmpA[:T, :], in_=ps[:T, pq:pq + 2, :].rearrange("p a b -> p (a b)"),
                                               func=AF.Copy, scale=0.125),
                 reads=["ps%d" % pq, "ps%d" % (pq + 1)], writes=RA)
            rope(tmpA[:T, :].rearrange("p (h c) -> p h c", h=16), RA, T, 16, cs[:T, t, :])
            xi = t % 3
            s.op("act", lambda e: e.activation(out=qbuf[xi][0][:T, :], in_=tmpA[:T, :], func=AF.Copy), reads=RA, writes=[qbuf[xi][1]])
            return xi

        def Q2(t, xi):
            b, pv = transpose8(T, lambda kc: qbuf[xi][0][:T, kc * 128:(kc + 1) * 128], [qbuf[xi][1]])
            s.op("act", lambda e: e.activation(out=qT[:, :, t * T:(t + 1) * T], in_=pv, func=AF.Copy),
                 reads=["ps%d" % b], writes=["qT%d" % t] + rBIG(0, 4096))

        qx = [Q1(0), Q1(1), Q1(2)]
        Q2(0, qx[0])
        if pre is not None:
            pre()
        qx.append(Q1(3))
        Q2(1, qx[1])
        Q2(2, qx[2])
        Q2(3, qx[3])

        def stA(t):
            has_prev = not (first_grp and t == 0)
            qres = ["qT%d" % t] + rBIG(0, 4096)
            cur_c0 = 144 + t * 128
            prev_c0 = 16 if t == 0 else 144 + (t - 1) * 128
            cur_slot, prev_slot = 2 + t, (1 if t == 0 else 1 + t)

            def kvh(k):
                par = k % 2
                bc, bp, bm, bo = par, 2 + par, 4 + par, 6 + par
                s.op("pe", lambda e: e.matmul(ps[:, bc, :], lhsT=ident[:, :], rhs=mbc[:, :], start=True, stop=False),
                     reads=["ident", "mbc"], writes=["ps%d" % bc], nosync_same=True)
                if has_prev:
                    s.op("pe", lambda e: e.matmul(ps[:, bp, :], lhsT=ident[:, :], rhs=mbp[:, :], start=True, stop=False),
                         reads=["ident", "mbp"], writes=["ps%d" % bp], nosync_same=True)
                for g in range(4):
                    pr, base = 2 * k + g // 2, (g % 2) * 64
                    qv = qT[base:base + 64, pr, t * T:(t + 1) * T]
                    s.op("pe", lambda e, g=g, base=base, qv=qv: e.matmul(
                        ps[:, bc, g * 128:(g + 1) * 128], lhsT=kT2[base:base + 64, k, cur_c0:cur_c0 + 128], rhs=qv,
                        start=False, stop=(g == 3)),
                        reads=qres + ["kT2_%d" % cur_slot], writes=["ps%d" % bc], nosync_same=True)
                    if has_prev:
                        s.op("pe", lambda e, g=g, base=base, qv=qv: e.matmul(
                            ps[:, bp, g * 128:(g + 1) * 128], lhsT=kT2[base:base + 64, k, prev_c0:prev_c0 + 128], rhs=qv,
                            start=False, stop=(g == 3)),
                            reads=qres + ["kT2_%d" % prev_slot], writes=["ps%d" % bp], nosync_same=True)
                    s.op("pe", lambda e, g=g, base=base, qv=qv: e.matmul(
                        ps[:16, bm, g * 128:(g + 1) * 128], lhsT=kT2[base:base + 64, k, 0:16], rhs=qv, start=True, stop=True),
                        reads=qres + ["kT2_0"], writes=["ps%d" % bm], nosync_same=True)
                pi = par * 3
                pres = lambda i: rBIG(4096 + (pi + i) * 512, 4096 + (pi + i + 1) * 512)
                s.op("act", lambda e: e.activation(out=pT[:, pi, :], in_=ps[:, bc, :], func=AF.Exp),
                     reads=["ps%d" % bc], writes=pres(0))
                if has_prev:
                    s.op("act", lambda e: e.activation(out=pT[:, pi + 1, :], in_=ps[:, bp, :], func=AF.Exp),
                         reads=["ps%d" % bp], writes=pres(1))
                s.op("act", lambda e: e.activation(out=pT[:16, pi + 2, :], in_=ps[:16, bm, :], func=AF.Exp),
                     reads=["ps%d" % bm], writes=pres(2))
            def kvp(k):
                par = k % 2
                bo = 6 + par
                pi = par * 3
                pres = lambda i: rBIG(4096 + (pi + i) * 512, 4096 + (pi + i + 1) * 512)
                for g in range(4):
                    oc = g * 65
                    s.op("pe", lambda e, g=g, oc=oc: e.matmul(
                        ps[:, bo, oc:oc + 65], lhsT=pT[:, pi, g * 128:(g + 1) * 128], rhs=vaug[:, cur_slot, k, :], start=True, stop=False),
                        reads=pres(0) + ["vaug%d" % cur_slot], writes=["ps%d" % bo], nosync_same=True)
                    if has_prev:
                        s.op("pe", lambda e, g=g, oc=oc: e.matmul(
                            ps[:, bo, oc:oc + 65], lhsT=pT[:, pi + 1, g * 128:(g + 1) * 128], rhs=vaug[:, prev_slot, k, :], start=False, stop=False),
                            reads=pres(1) + ["vaug%d" % prev_slot], writes=["ps%d" % bo], nosync_same=True)
                    s.op("pe", lambda e, g=g, oc=oc: e.matmul(
                        ps[:, bo, oc:oc + 65], lhsT=pT[:16, pi + 2, g * 128:(g + 1) * 128], rhs=vaug[:16, 0, k, :], start=False, stop=True),
                        reads=pres(2) + ["vaug0"], writes=["ps%d" % bo], nosync_same=True)
                ov = ps[:, bo, 0:260].rearrange("p (h c) -> p h c", h=4)
                s.op("dve", lambda e: e.tensor_tensor(out=den[:, 0, :].unsqueeze(2), in0=ov[:, :, 64:65],
                                                      in1=esink[:, 4 * k:4 * k + 4].unsqueeze(2), op=ALU.add),
                     reads=["ps%d" % bo, "esink"], writes=["den"])
                s.op("dve", lambda e: e.reciprocal(out=den[:, 1, :], in_=den[:, 0, :]), reads=["den"], writes=["rden"])
                s.op("dve", lambda e: e.tensor_tensor(
                    out=ob2[:, t % 2, k * 256:(k + 1) * 256].rearrange("p (h c) -> p h c", h=4), in0=ov[:, :, 0:64],
                    in1=den[:, 1, :].unsqueeze(2).to_broadcast([128, 4, 64]), op=ALU.mult),
                    reads=["ps%d" % bo, "rden"], writes=["ob%d" % (t % 2)])

            kvh(0)
            for k_ in range(1, 4):
                kvh(k_)
                kvp(k_ - 1)
            kvp(3)

        def stN(t):
            pass

        def stP(t):
            b, pv = transpose8(T, lambda kc: ob2[:T, t % 2, kc * 128:(kc + 1) * 128], ["ob%d" % (t % 2)])
            s.op("act", lambda e: e.activation(out=yTb[:, :, 0:T], in_=pv, func=AF.Copy), reads=["ps%d" % b], writes=["yTb"])
            pb = pairB()
            mm_wout(T, lambda j: yTb[:, j, 0:T], ["yTb"], 8, pb)
            postnorm(t, T, pb, False)

        return pipeline(NT, stA, stN, stP, nxt)

    def load_x(sq, g, t):
        t0 = g * G
        s.dma("pool", lambda e: e.dma_start(out=h[:, t, :], in_=x_d[sq, t0 + t * 128:t0 + (t + 1) * 128, :]),
              "hld%d" % t, writes=["h%d" % t])

    def load_cs(g):
        t0 = g * G
        s.dma("pool", lambda e: e.dma_start(out=cs[:, :, :], in_=cs_d[NMETA + t0:NMETA + t0 + G, :].rearrange("(t p) c -> p t c", p=128)),
              "cs", writes=["cs"])

    def store_y(sq, g, t):
        t0 = g * G
        s.dma("pool", lambda e: e.dma_start(out=y_d[sq, t0 + t * 128:t0 + (t + 1) * 128, :], in_=h[:, t, :]),
              "hst%d" % t, reads=["h%d" % t])

    def carry_kv():
        s.op("pool", lambda e: e.tensor_copy(out=kT2[:, :, 16:144], in_=kT2[:, :, 528:656]),
             reads=["kT2_5"], writes=["kT2_1"])
        s.op("pool", lambda e: e.tensor_copy(out=vaug[:, 1, :, 0:64], in_=vaug[:, 5, :, 0:64]),
             reads=["vaug5"], writes=["vaug1"])

    groups = [(sq, g) for sq in range(n_seq) for g in range(n_grp)]

    def boundary_nxt(i):
        sq, g = groups[i]
        if i + 1 < len(groups):
            nsq, ng = groups[i + 1]

            def before(t):
                store_y(sq, g, t)
                load_x(nsq, ng, t)
            return Nxt([(XnT, "XnT", 0)], before)
        return None

    XN = lambda gi: Nxt([(XnT, "XnT", gi)])

    s.dma("pool", lambda e: e.dma_start(out=h[:16, 4, :], in_=meta_d), "hld4", writes=["h4"])
    for t in range(4):
        load_x(0, 0, t)
    load_cs(0)
    cast_ffn(0)
    cast_gla()
    d = ffn(0, 0, 1, 128, 4, 0, True, XN(2))
    cast_ffn(1)
    cast_kv()
    ffn(0, 0, 1, 16, 1, 4, True, None, XnT=XnTm, xp="XnTm")
    load_gla_consts()
    gla(16, 1, 4, True, None, XnT=XnTm, xp="XnTm")
    for hd in range(4):
        s.op("act", lambda e, hd=hd: e.activation(out=Sm[:, hd, :], in_=S[:, hd, :], func=AF.Copy), reads=["S%d" % hd], writes=["Sm%d" % hd])
        s.op("pool", lambda e, hd=hd: e.tensor_copy(out=Smb[:, hd, :], in_=Sb[:, hd, :]), reads=["Sb%d" % hd], writes=["Smb%d" % hd])
    cast_ffn(2)
    d = gla(128, 4, 0, False, XN(4), pre=d)
    ffn(1, 4, 5, 16, 1, 4, True, None, XnT=XnTm, xp="XnTm")
    cast_swa()
    cast_ffn(3)
    d = ffn(1, 4, 5, 128, 4, 0, False, Nxt([(XnTk, "XnTk", 12), (XnT, "XnT", 6)]), pre=d)
    kv(16, 1, 4, 0, 0, True, lambda t: csm[:16, :], XnTk=XnTkm, xp="XnTkm")
    kv(128, 4, 0, 2, 144, False, lambda t: cs[:128, t, :], pre=d)
    d = ffn(2, 6, 7, 128, 4, 0, False, XN(8))
    d = swa(True, False, XN(10), pre=d)
    d = ffn(3, 10, 11, 128, 4, 0, False, boundary_nxt(0), pre=d)
    if len(groups) == 1:
        for t in range(4):
            store_y(0, 0, t)
    for i in range(1, len(groups)):
        sq, g = groups[i]
        if g == 0:
            for hd in range(4):
                s.op("act", lambda e, hd=hd: e.activation(out=S[:, hd, :], in_=Sm[:, hd, :], func=AF.Copy), reads=["Sm%d" % hd], writes=["S%d" % hd])
                s.op("pool", lambda e, hd=hd: e.tensor_copy(out=Sb[:, hd, :], in_=Smb[:, hd, :]), reads=["Smb%d" % hd], writes=["Sb%d" % hd])
        else:
            carry_kv()
        load_cs(g)
        d = ffn(0, 0, 1, 128, 4, 0, False, XN(2), pre=d)
        d = gla(128, 4, 0, False, XN(4), pre=d)
        d = ffn(1, 4, 5, 128, 4, 0, False, Nxt([(XnTk, "XnTk", 12), (XnT, "XnT", 6)]), pre=d)
        kv(128, 4, 0, 2, 144, False, lambda t: cs[:128, t, :], pre=d)
        d = ffn(2, 6, 7, 128, 4, 0, False, XN(8))
        d = swa(g == 0, False, XN(10), pre=d)
        nb = boundary_nxt(i)
        d = ffn(3, 10, 11, 128, 4, 0, False, nb, pre=d)
        if nb is None:
            for t in range(4):
                store_y(sq, g, t)
    s.emit()
    st.close()
    return nc


def host_inputs(inp):
    f = lambda a: np.ascontiguousarray(np.asarray(a, dtype=np.float32))
    gains = np.concatenate([f(inp["norm_gains"]).reshape(12, D), f(inp["kv_norm"]).reshape(1, D)], 0)
    gT = gains.reshape(13, 8, 128).transpose(2, 0, 1).reshape(128, 13 * 8)
    bf = ml_dtypes.bfloat16
    i = np.arange(128)
    masku = (i[None, :] >= i[:, None]).astype(np.float32)
    mb_cur = np.where(i[:, None] <= i[None, :], 0.0, NEG).astype(np.float32)
    mb_prev = np.where(i[:, None] > i[None, :], 0.0, NEG).astype(np.float32)
    rstm = np.ones((128, 512), np.float32)
    rstm[:, ::128] = 0.0
    inv = (500000.0 ** (-np.arange(0, 16, 2, dtype=np.float32) / 16)).astype(np.float32)
    ang = np.arange(NMETA + SEQ, dtype=np.float32)[:, None] * inv[None, :]
    cs = np.concatenate([np.cos(ang), np.sin(ang)], 1).astype(np.float32)
    common = {
        "meta": f(inp["meta_tokens"]), "gT": np.ascontiguousarray(gT), "gains": gains,
        "w_ffn_in": f(inp["w_ffn_in"]).reshape(4, D, 2 * DFF), "w_ffn_out": f(inp["w_ffn_out"]).reshape(4, DFF, D),
        "gla_w_in": f(inp["gla_w_in"])[0], "gla_w_gate": f(inp["gla_w_gate"])[0],
        "gla_b4": np.ascontiguousarray(f(inp["gla_b_gate"])[0].reshape(4, 128).T),
        "gla_norm": f(inp["gla_norm"]).reshape(1, 256), "gla_w_out": f(inp["gla_w_out"])[0],
        "w_kv": f(inp["w_kv"]), "swa_w_q": f(inp["swa_w_q"])[0], "swa_sinks": f(inp["swa_sinks"]).reshape(1, 16),
        "swa_w_out": f(inp["swa_w_out"])[0],
        "ident": np.eye(128, dtype=np.float32).astype(bf), "masku": masku.astype(bf),
        "mb_cur": np.tile(mb_cur, (1, 4)).astype(bf), "mb_prev": np.tile(mb_prev, (1, 4)).astype(bf),
        "rstmask": rstm.astype(bf), "cs": cs,
    }
    return common


_NC_CACHE = {}


def kernel(**inputs):
    common = host_inputs(inputs)
    x = np.ascontiguousarray(np.asarray(inputs["x"], dtype=np.float32))
    if "nc" not in _NC_CACHE:
        _NC_CACHE["nc"] = build()
    nc = _NC_CACHE["nc"]
    in_maps = [dict(common, x=x[2 * c:2 * c + 2]) for c in range(8)]
    res = run_bass_kernel_spmd(nc, in_maps, core_ids=list(range(8)))
    return np.concatenate([np.asarray(r["y"], dtype=np.float32) for r in res.results], axis=0)
```

```python
import numpy as np
import ml_dtypes
from contextlib import ExitStack
import concourse.bass as bass
import concourse.mybir as mybir
from concourse.bass_utils import run_bass_kernel_spmd

F32 = mybir.dt.float32
BF16 = mybir.dt.bfloat16
AF = mybir.ActivationFunctionType
ALU = mybir.AluOpType

D = 1024
DFF = 2816
NJ = 22
NMETA = 16
SEQ = 2048
G = 512
EPS = 1e-6
NEG = -30000.0
ENGS = ("pe", "act", "dve", "pool", "sp")
import os
PROF = bool(os.environ.get("KPROF"))
TAGS = {}


class Sched:
    def __init__(self, nc):
        self.nc = nc
        self.ops = {e: [] for e in ENGS}
        self.cnt = {}
        self.res_w = {}
        self.res_r = {}
        self.seen = {e: {} for e in ENGS}

    def _deps(self, eng, reads, writes, nosync_same):
        need = {}

        def add(tok):
            if tok is None:
                return
            k, v = tok
            if k == eng and nosync_same:
                return
            if need.get(k, 0) < v:
                need[k] = v

        for r in reads:
            add(self.res_w.get(r))
            if r.startswith("ps"):
                for t in self.res_r.get(r, ()):
                    if t[0] != eng:
                        add(t)
        for w in writes:
            add(self.res_w.get(w))
            for t in self.res_r.get(w, ()):
                add(t)
        waits = []
        seen = self.seen[eng]
        for k, v in need.items():
            if seen.get(k, 0) >= v:
                continue
            seen[k] = v
            waits.append((k, v))
        return waits

    def _commit(self, tok, reads, writes):
        for r in reads:
            self.res_r.setdefault(r, []).append(tok)
        for w in writes:
            self.res_w[w] = tok
            self.res_r[w] = []

    @staticmethod
    def _tag():
        import sys
        f = sys._getframe(2)
        names = []
        while f is not None and f.f_code.co_name != "build":
            names.append("%s:%d" % (f.f_code.co_name, f.f_lineno))
            f = f.f_back
        return ">".join(reversed(names))

    def op(self, eng, fn, reads=(), writes=(), nosync_same=False):
        waits = self._deps(eng, reads, writes, nosync_same)
        v = self.cnt.get(eng, 0) + 1
        self.cnt[eng] = v
        self.ops[eng].append((waits, fn, (eng, 1), self._tag() if PROF else None))
        self._commit((eng, v), reads, writes)

    def dma(self, queue, fn, semkey, reads=(), writes=()):
        waits = self._deps(queue, reads, writes, False)
        v = self.cnt.get(semkey, 0) + 16
        self.cnt[semkey] = v
        self.ops[queue].append((waits, fn, (semkey, 16), None))
        self._commit((semkey, v), reads, writes)

    def emit(self, final_wait_engine="pool"):
        nc = self.nc
        with ExitStack() as st:
            sems = {}
            for i, k in enumerate(self.cnt):
                sems[k] = st.enter_context(nc.semaphore("s%d" % i))
            fin = [(k, v) for k, v in self.cnt.items() if k not in ENGS]
            block = st.enter_context(nc.Block())

            def run(name, eng):
                for waits, fn, inc, tag in self.ops[name]:
                    for k, v in waits:
                        eng.wait_ge(sems[k], v)
                    ins = fn(eng)
                    ins.then_inc(sems[inc[0]], inc[1])
                    if tag is not None:
                        TAGS[ins.ins.name] = tag
                if name == final_wait_engine:
                    for k, v in fin:
                        eng.wait_ge(sems[k], v)

            @block.tensor
            def _(e):
                run("pe", e)

            @block.scalar
            def _(e):
                run("act", e)

            @block.vector
            def _(e):
                run("dve", e)

            @block.gpsimd
            def _(e):
                run("pool", e)

            @block.sync
            def _(e):
                run("sp", e)


def build(n_seq=2, n_grp=4):
    nc = bass.Bass("TRN2", target_bir_lowering=False)
    din = lambda n, sh, dt=F32: nc.dram_tensor(n, sh, dt, kind="ExternalInput").ap()
    dscr = lambda n, sh, dt=BF16: nc.dram_tensor(n, sh, dt, kind="Internal").ap()
    x_d = din("x", [n_seq, SEQ, D])
    meta_d = din("meta", [NMETA, D])
    gT_d = din("gT", [128, 13 * 8])
    gains_d = din("gains", [13, D])
    wi_d = din("w_ffn_in", [4, D, 2 * DFF])
    wo_d = din("w_ffn_out", [4, DFF, D])
    gwi_d = din("gla_w_in", [D, 3088])
    gwg_d = din("gla_w_gate", [16, 512])
    gb_d = din("gla_b4", [128, 4])
    gn_d = din("gla_norm", [1, 256])
    gwo_d = din("gla_w_out", [D, D])
    wkv_d = din("w_kv", [D, 512])
    wq_d = din("swa_w_q", [D, D])
    sink_d = din("swa_sinks", [1, 16])
    swo_d = din("swa_w_out", [D, D])
    ident_d = din("ident", [128, 128], BF16)
    masku_d = din("masku", [128, 128], BF16)
    mbc_d = din("mb_cur", [128, 512], BF16)
    mbp_d = din("mb_prev", [128, 512], BF16)
    rst_d = din("rstmask", [128, 512], BF16)
    cs_d = din("cs", [NMETA + SEQ, 16])
    y_d = nc.dram_tensor("y", [n_seq, SEQ, D], F32, kind="ExternalOutput").ap()

    wi_b = dscr("wi_b", [4, D, 2 * DFF])
    wo_b = dscr("wo_b", [4, DFF, D])
    gwi_b = dscr("gwi_b", [D, 3088])
    gwg_b = dscr("gwg_b", [16, 512])
    gwo_b = dscr("gwo_b", [D, D])
    wkv_b = dscr("wkv_b", [D, 512])
    wq_b = dscr("wq_b", [D, D])
    swo_b = dscr("swo_b", [D, D])

    s = Sched(nc)
    st = ExitStack()
    sb = lambda n, sh, dt: st.enter_context(nc.sbuf_tensor(n, sh, dt))
    h = sb("h", [128, 5, D], F32)
    xs = sb("xs", [128, 3, D], BF16)
    XnT = sb("XnT", [128, 8, G], BF16)
    XnTk = sb("XnTk", [128, 8, G], BF16)
    XnTm = sb("XnTm", [128, 8, 16], BF16)
    XnTkm = sb("XnTkm", [128, 8, 16], BF16)
    BIG = sb("BIG", [128, 12288], BF16)
    wblk = sb("wblk", [128, 4, 8, 512], BF16)
    wout = sb("wout", [128, NJ, D], BF16)
    gB = sb("gB", [128, 2, D], F32)
    tmp = sb("tmp", [128, 4, 512], F32)
    S = sb("S", [128, 4, 256], F32)
    Sb = sb("Sb", [128, 4, 256], BF16)
    Sm = sb("Sm", [128, 4, 256], F32)
    Smb = sb("Smb", [128, 4, 256], BF16)
    kT2 = sb("kT2", [128, 4, 656], BF16)
    vaug = sb("vaug", [128, 6, 4, 65], BF16)
    ident = sb("identS", [128, 128], BF16)
    masku = sb("maskuS", [128, 128], BF16)
    mbc = sb("mbcS", [128, 512], BF16)
    mbp = sb("mbpS", [128, 512], BF16)
    rst = sb("rstS", [128, 512], BF16)
    gT = sb("gTS", [128, 13 * 8], F32)
    negb = sb("negb", [128, 4], F32)
    wgate = sb("wgate", [16, 512], BF16)
    wlr = sb("wlr", [128, 8, 16], BF16)
    cs = sb("csS", [128, 4, 16], F32)
    csm = sb("csm", [16, 16], F32)
    esink = sb("esink", [128, 16], F32)
    ghead = sb("ghead", [128, 256], F32)
    small = sb("small", [128, 64], F32)
    dec = sb("dec", [128, 4, 4], F32)
    nlast = sb("nlast", [128, 4, 4], F32)
    attsb = sb("attsb", [128, 2, 128], BF16)
    kf = sb("kf", [128, 256], F32)
    kdup = sb("kdup", [128, 4, 4, 2, 64], BF16)
    rt = sb("rt", [128, 4, 128], F32)
    sg = sb("sg", [128, 2, 512], F32)
    junk = sg[:, 0, :].bitcast(BF16)
    den = sb("den", [128, 2, 4], F32)
    srb2 = sb("srb2", [128, 2, D], BF16)
    ob2 = sb("ob2", [128, 2, D], BF16)
    yTb = sb("yTb", [128, 8, 128], BF16)
    ps = st.enter_context(nc.psum_tensor("ps", [128, 8, 512], F32))

    GT = BIG[:, 0:NJ * G].rearrange("p (j t) -> p j t", j=NJ)
    qd = BIG[:, 0:2048].rearrange("p (h t) -> p h t", h=4)
    kd = BIG[:, 2048:4096].rearrange("p (h t) -> p h t", h=4)
    ke = BIG[:, 4096:6144].rearrange("p (h t) -> p h t", h=4)
    vtok = BIG[:, 6144:10240].rearrange("p (t c) -> p t c", t=4)
    kendt = BIG[:, 11264:11776]
    lrT = BIG[:, 11776:12288]
    qT = BIG[:, 0:4096].rearrange("p (k t) -> p k t", k=8)
    pT = BIG[:, 4096:4096 + 6 * 512].rearrange("p (s t) -> p s t", s=6)
    tmpA = tmp[:, 0:2, :].rearrange("p a b -> p (a b)")
    tmpB = tmp[:, 2:4, :].rearrange("p a b -> p (a b)")
    RA = ["tq0", "tq1"]
    RB = ["tq2", "tq3"]
    rBIG = lambda a, b: ["big%d" % i for i in range(a // 512, (b + 511) // 512)]
    P0, P1 = 4, 6
    epsT = small[:, 63:64]

    def psbf(b):
        return ps[:, b, :].bitcast(BF16)

    state = {"bankA": 0, "pairB": 0, "wslot": 0, "xs": 0, "sm": 0, "gb": 0}

    def bankA():
        b = state["bankA"]
        state["bankA"] = (b + 1) % 4
        return b

    def pairB():
        b = state["pairB"]
        state["pairB"] = 1 - b
        return 4 + 2 * b

    def wslot():
        b = state["wslot"]
        state["wslot"] = (b + 1) % 4
        return b

    def xslot():
        b = state["xs"]
        state["xs"] = 1 - b
        return b

    def smcol():
        b = state["sm"]
        state["sm"] = 1 - b
        return 24 * b

    def cload(dst, src, key):
        s.dma("pool", lambda e: e.dma_start(out=dst, in_=src), key, writes=[key])

    cload(ident[:], ident_d, "ident")
    cload(gT[:], gT_d, "gT")
    cload(masku[:], masku_d, "masku")
    cload(mbc[:], mbc_d, "mbc")
    cload(mbp[:], mbp_d, "mbp")
    cload(rst[:], rst_d, "rst")
    cload(negb[:], gb_d, "negb_raw")
    cload(esink[:], sink_d.partition_broadcast(128), "esink_raw")
    cload(ghead[:], gn_d.partition_broadcast(128), "ghead")
    cload(csm[:], cs_d[0:16, :], "csm")
    s.op("dve", lambda e: e.tensor_scalar(out=negb[:], in0=negb[:], scalar1=-1.0, scalar2=None, op0=ALU.mult),
         reads=["negb_raw"], writes=["negb"])
    s.op("act", lambda e: e.activation(out=esink[:], in_=esink[:], func=AF.Exp), reads=["esink_raw"], writes=["esink"])
    s.op("pool", lambda e: e.memset(small[:, 62:63], float(np.log(0.5))), writes=["lnhalf"])
    s.op("pool", lambda e: e.memset(small[:, 63:64], EPS), reads=["lnhalf"], writes=["eps"])
    s.op("pool", lambda e: e.memset(vaug[:], 1.0), writes=["vaug_init"])
    s.op("pool", lambda e: e.memset(S[:], 0.0), writes=["S0", "S1", "S2", "S3"])
    s.op("pool", lambda e: e.memset(Sb[:], 0.0), writes=["Sb0", "Sb1", "Sb2", "Sb3"])

    cast_hist = []

    def cast(dst, src, key):
        after = cast_hist[-3:-2]
        s.dma("pool", lambda e: e.dma_start(out=dst, in_=src), key, reads=after, writes=[key])
        cast_hist.append(key)

    SB4 = DFF // 2

    def wi_keys(m, c0, ncol):
        return sorted({"c_wi%d_%d" % (m, c // SB4) for c in (c0, c0 + ncol - 1)})

    def cast_ffn(m):
        for q in (0, 2, 1, 3):
            cast(wi_b[m, :, q * SB4:(q + 1) * SB4], wi_d[m, :, q * SB4:(q + 1) * SB4], "c_wi%d_%d" % (m, q))
        cast(wo_b[m], wo_d[m], "c_wo%d" % m)

    def cast_gla():
        cast(gwg_b, gwg_d, "c_gwg")
        cast(gwi_b[:, 3072:3088], gwi_d[:, 3072:3088], "c_gwi6")
        for q in range(6):
            cast(gwi_b[:, q * 512:(q + 1) * 512], gwi_d[:, q * 512:(q + 1) * 512], "c_gwi%d" % q)
        cast(gwo_b, gwo_d, "c_gwo")

    def load_gla_consts():
        s.dma("sp", lambda e: e.dma_start(out=wgate[:], in_=gwg_b), "wgate", reads=["c_gwg"], writes=["wgate"])
        s.dma("sp", lambda e: e.dma_start(out=wlr[:], in_=gwi_b.rearrange("(kc p) c -> p kc c", p=128)[:, :, 3072:3088]),
              "wlr", reads=["c_gwi6"], writes=["wlr"])

    def cast_kv():
        cast(wkv_b, wkv_d, "c_wkv")

    def cast_swa():
        cast(wq_b, wq_d, "c_wq")
        cast(swo_b, swo_d, "c_swo")

    def load_blk(src2d, c0, ncol, casts):
        sl = wslot()
        key = "wblk%d" % sl
        s.dma("sp", lambda e: e.dma_start(out=wblk[:, sl, :, 0:ncol],
                                          in_=src2d.rearrange("(kc p) c -> p kc c", p=128)[:, :, c0:c0 + ncol]),
              key, reads=casts, writes=[key])
        return sl, key

    def load_wout(src2d, nchunk, casts):
        v = src2d.rearrange("(j p) c -> p j c", p=128)
        for p0 in range(0, nchunk, 2):
            key = "wout%d" % (p0 // 2)
            s.dma("sp", lambda e, p0=p0: e.dma_start(out=wout[:, p0:p0 + 2, :], in_=v[:, p0:p0 + 2, :]),
                  key, reads=casts, writes=[key])

    def load_gB(gi):
        gs_ = 1 - state["gb"]
        state["gb"] = gs_
        s.dma("sp", lambda e: e.dma_start(out=gB[:, gs_, :], in_=gains_d[gi:gi + 1, :].partition_broadcast(128)),
              "gB%d" % gs_, writes=["gB%d" % gs_])

    def rstd_from_ss(col, T, n, scale, half=False):
        s.op("act", lambda e: e.activation(out=small[:T, col + 8:col + 8 + n], in_=small[:T, col:col + n], func=AF.Ln,
                                           scale=scale, bias=epsT[:T, :]),
             reads=["ss%d" % col, "eps"], writes=["ln%d" % col])
        if half:
            s.op("act", lambda e: e.activation(out=small[:T, col + 16:col + 16 + n], in_=small[:T, col + 8:col + 8 + n],
                                               func=AF.Exp, scale=-0.5, bias=small[:T, 62:63]),
                 reads=["ln%d" % col, "eps"], writes=["rstd%d" % col])
        else:
            s.op("act", lambda e: e.activation(out=small[:T, col + 16:col + 16 + n], in_=small[:T, col + 8:col + 8 + n],
                                               func=AF.Exp, scale=-0.5),
                 reads=["ln%d" % col], writes=["rstd%d" % col])

    def pre_A(T, ht):
        c = smcol()
        s.op("act", lambda e: e.activation(out=junk[:T, :], in_=h[:T, ht, :], func=AF.Square, accum_out=small[:T, c:c + 1]),
             reads=["h%d" % ht], writes=["ss%d" % c, "junk", "sg0"])
        rstd_from_ss(c, T, 1, 1.0 / D)
        xi = 2 if ht == 4 else xslot()
        s.op("act", lambda e: e.activation(out=xs[:T, xi, :], in_=h[:T, ht, :], func=AF.Copy, scale=small[:T, c + 16:c + 17]),
             reads=["h%d" % ht, "rstd%d" % c], writes=["xs%d" % xi])
        return xi

    def transpose8(T, src_fn, src_res):
        b = bankA()
        pv = psbf(b)[:, 0:8 * T].rearrange("p (k t) -> p k t", k=8)
        for kc in range(8):
            s.op("pe", lambda e, kc=kc: e.transpose(out=pv[:, kc, :], in_=src_fn(kc), identity=ident[:T, :T]),
                 reads=src_res + ["ident"], writes=["ps%d" % b], nosync_same=True)
        return b, pv

    def pre_B(T, t, xi, outs):
        b, pv = transpose8(T, lambda kc: xs[:T, xi, kc * 128:(kc + 1) * 128], ["xs%d" % xi])
        for buf, rp, gi in outs:
            s.op("dve", lambda e, buf=buf, gi=gi: e.tensor_tensor(
                out=buf[:, :, t * T:(t + 1) * T], in0=pv,
                in1=gT[:, gi * 8:(gi + 1) * 8].unsqueeze(2).to_broadcast([128, 8, T]), op=ALU.mult),
                reads=["ps%d" % b, "gT"], writes=["%s%d" % (rp, t)])

    def own_prenorm(T, NT, h0, outs):
        pend = None
        for t in range(NT):
            xi = pre_A(T, h0 + t)
            if pend is not None:
                pre_B(T, *pend, outs)
            pend = (t, xi)
        pre_B(T, *pend, outs)

    class Nxt:
        def __init__(self, outs, before=None):
            self.outs = outs
            self.before = before

        def A(self, t):
            if self.before is not None:
                self.before(t)
            return pre_A(128, t)

        def B(self, t, xi):
            pre_B(128, t, xi, self.outs)

    def pipeline(NT, stA, stN, stP, nxt):
        pend = {}
        stA(0)
        stN(0)
        for t in range(1, NT):
            stA(t)
            if nxt is not None and t >= 2:
                nxt.B(t - 2, pend.pop(t - 2))
            stP(t - 1)
            stN(t)
            if nxt is not None:
                pend[t - 1] = nxt.A(t - 1)
        stP(NT - 1)
        if nxt is not None and NT >= 2:
            nxt.B(NT - 2, pend.pop(NT - 2))
        if nxt is not None:
            xi = nxt.A(NT - 1)
            return lambda: nxt.B(NT - 1, xi)
        return None

    def mm_tokmajor(T, lhs_fn, lhs_res, blocks, pb):
        for half in range(2):
            sl, key = blocks[half]
            for kc in range(8):
                s.op("pe", lambda e, kc=kc, half=half, sl=sl: e.matmul(
                    ps[:T, pb + half, :], lhsT=lhs_fn(kc), rhs=wblk[:, sl, kc, :], start=(kc == 0), stop=(kc == 7)),
                    reads=lhs_res + [key], writes=["ps%d" % (pb + half)], nosync_same=True)

    def mm_wout(T, lhs_fn, lhs_res, nchunk, pb, mid=None):
        for half in range(2):
            if half == 1 and mid is not None:
                mid()
            for j in range(nchunk):
                s.op("pe", lambda e, j=j, half=half: e.matmul(
                    ps[:T, pb + half, :], lhsT=lhs_fn(j), rhs=wout[:, j, half * 512:(half + 1) * 512],
                    start=(j == 0), stop=(j == nchunk - 1)),
                    reads=lhs_res + ["wout%d" % (j // 2)], writes=["ps%d" % (pb + half)], nosync_same=True)

    def postnorm(ht, T, pb, half):
        c = smcol()
        src = ps[:T, pb:pb + 2, :].rearrange("p a b -> p (a b)")
        s.op("act", lambda e: e.activation(out=junk[:T, :], in_=src, func=AF.Square, accum_out=small[:T, c:c + 1]),
             reads=["ps%d" % pb, "ps%d" % (pb + 1)], writes=["ss%d" % c, "junk", "sg0"])
        gs_ = state["gb"]
        s.op("dve", lambda e: e.tensor_tensor(out=tmpA[:T, :], in0=src, in1=gB[:T, gs_, :], op=ALU.mult),
             reads=["ps%d" % pb, "ps%d" % (pb + 1), "gB%d" % gs_], writes=RA)
        rstd_from_ss(c, T, 1, 1.0 / D, half=half)
        s.op("dve", lambda e: e.scalar_tensor_tensor(out=h[:T, ht, :], in0=tmpA[:T, :], scalar=small[:T, c + 16:c + 17], in1=h[:T, ht, :],
                                                     op0=ALU.mult, op1=ALU.add),
             reads=RA + ["rstd%d" % c, "h%d" % ht], writes=["h%d" % ht])

    def ffn(m, gi_pre, gi_post, T, NT, h0, own_pre, nxt, XnT=XnT, xp="XnT", pre=None):
        if pre is not None and "F" in os.environ.get("KDBG", ""):
            pre()
            pre = None
        GW = T * NT
        if own_pre:
            own_prenorm(T, NT, h0, [(XnT, xp, gi_pre)])
        xres = [xp + "%d" % t for t in range(NT)]
        nblk = 6
        pend = []

        def issue(bi):
            c0 = bi * 512
            ncol = min(512, DFF - c0)
            g_ = load_blk(wi_b[m], c0, ncol, wi_keys(m, c0, ncol))
            u_ = load_blk(wi_b[m], DFF + c0, ncol, wi_keys(m, DFF + c0, ncol))
            pend.append((g_, u_, ncol))

        issue(0)
        issue(1)
        load_gB(gi_post)
        for bi in range(nblk):
            (gs, gk), (us, uk), ncol = pend[bi]
            if bi == 0:
                load_wout(wo_b[m], NJ, ["c_wo%d" % m])
            def chunk(jj, c0, c1, bi=bi, gs=gs, gk=gk, us=us, uk=uk):
                j = bi * 4 + jj
                bg, bu = (0, 1) if j % 2 == 0 else (2, 3)
                xr = [xp + "%d" % t for t in range(c0 // T, (c1 + T - 1) // T)]
                for kc in range(8):
                    s.op("pe", lambda e, kc=kc: e.matmul(
                        ps[:, bg, c0:c1], lhsT=wblk[:, gs, kc, jj * 128:(jj + 1) * 128], rhs=XnT[:, kc, c0:c1],
                        start=(kc == 0), stop=(kc == 7)),
                        reads=xr + [gk], writes=["ps%d" % bg], nosync_same=True)
                for kc in range(8):
                    s.op("pe", lambda e, kc=kc: e.matmul(
                        ps[:, bu, c0:c1], lhsT=wblk[:, us, kc, jj * 128:(jj + 1) * 128], rhs=XnT[:, kc, c0:c1],
                        start=(kc == 0), stop=(kc == 7)),
                        reads=xr + [uk], writes=["ps%d" % bu], nosync_same=True)
                si = j % 2
                s.op("act", lambda e: e.activation(out=sg[:, si, c0:c1], in_=ps[:, bg, c0:c1], func=AF.Silu),
                     reads=["ps%d" % bg], writes=["sg%d" % si])
                s.op("dve", lambda e: e.tensor_tensor(out=GT[:, j, c0:c1], in0=sg[:, si, c0:c1], in1=ps[:, bu, c0:c1], op=ALU.mult),
                     reads=["sg%d" % si, "ps%d" % bu], writes=rBIG(j * 512, j * 512 + 512))

            if bi == 0 and pre is not None and NT == 4:
                for jj in range(4):
                    chunk(jj, 0, 3 * T)
                pre()
                for jj in range(4):
                    chunk(jj, 3 * T, 4 * T)
            else:
                if bi == 0 and pre is not None:
                    pre()
                for jj in range(ncol // 128):
                    chunk(jj, 0, GW)
            if bi + 2 < nblk:
                issue(bi + 2)
        gtres = rBIG(0, NJ * 512)
        pendq = []
        for t in range(NT):
            pb = pairB()
            mid = None
            if len(pendq) == 2:
                p_ = pendq.pop(0)
                mid = lambda p_=p_: nxt.B(*p_)
            mm_wout(T, lambda j, t=t: GT[:, j, t * T:(t + 1) * T], gtres, NJ, pb, mid)
            postnorm(h0 + t, T, pb, True)
            if nxt is not None:
                pendq.append((t, nxt.A(t)))
        if pendq:
            while len(pendq) > 1:
                nxt.B(*pendq.pop(0))
            last = pendq[0]
            return lambda: nxt.B(*last)
        return None

    def gla(T, NT, h0, own_pre, nxt, XnT=XnT, xp="XnT", pre=None):
        if pre is not None and "G" in os.environ.get("KDBG", ""):
            pre()
            pre = None
        GW = T * NT
        if own_pre:
            own_prenorm(T, NT, h0, [(XnT, xp, 2)])
        load_gB(3)
        xres = [xp + "%d" % t for t in range(NT)]
        bv0 = load_blk(gwi_b, 1024, 512, ["c_gwi2"])
        bv1 = load_blk(gwi_b, 1536, 512, ["c_gwi3"])
        bq = load_blk(gwi_b, 0, 512, ["c_gwi0"])
        bk = load_blk(gwi_b, 512, 512, ["c_gwi1"])
        load_wout(gwo_b, 8, ["c_gwo"])

        def vproj(t):
            pb = pairB()
            mm_tokmajor(T, lambda kc: XnT[:, kc, t * T:(t + 1) * T], [xp + "%d" % t], [bv0, bv1], pb)
            s.op("act", lambda e: e.activation(out=vtok[:T, t, :], in_=ps[:T, pb:pb + 2, :].rearrange("p a b -> p (a b)"),
                                               func=AF.Copy),
                 reads=["ps%d" % pb, "ps%d" % (pb + 1)], writes=rBIG(6144 + t * 1024, 6144 + t * 1024 + 1024))

        for t_ in range(NT - 1):
            vproj(t_)
        if pre is not None:
            pre()
        vproj(NT - 1)
        b = bankA()
        for kc in range(8):
            s.op("pe", lambda e, kc=kc, b=b: e.matmul(ps[:16, b, 0:GW], lhsT=wlr[:, kc, :], rhs=XnT[:, kc, 0:GW],
                                                      start=(kc == 0), stop=(kc == 7)),
                 reads=xres + ["wlr"], writes=["ps%d" % b], nosync_same=True)
        s.op("act", lambda e, b=b: e.activation(out=lrT[:16, 0:GW], in_=ps[:16, b, 0:GW], func=AF.Copy),
             reads=["ps%d" % b], writes=rBIG(11776, 12288))
        Lr, Eq, Ek, Ee = tmp[:, 0, :], tmp[:, 1, :], tmp[:, 2, :], tmp[:, 3, :]

        gstate = {"b": 0}

        def gbank():
            b_ = gstate["b"]
            gstate["b"] = (b_ + 1) % 8
            return b_

        def gate_head(hd):
            b = gbank()
            s.op("pe", lambda e: e.matmul(ps[:, b, 0:GW], lhsT=wgate[:, hd * 128:(hd + 1) * 128], rhs=lrT[:16, 0:GW],
                                          start=True, stop=True),
                 reads=rBIG(11776, 12288) + ["wgate"], writes=["ps%d" % b], nosync_same=True)
            b1 = gbank()
            for kc in range(8):
                s.op("pe", lambda e, kc=kc: e.matmul(ps[:, b1, 0:GW], lhsT=wblk[:, bq[0], kc, hd * 128:(hd + 1) * 128],
                                                     rhs=XnT[:, kc, 0:GW], start=(kc == 0), stop=(kc == 7)),
                     reads=xres + [bq[1]], writes=["ps%d" % b1], nosync_same=True)
            b2 = gbank()
            for kc in range(8):
                s.op("pe", lambda e, kc=kc: e.matmul(ps[:, b2, 0:GW], lhsT=wblk[:, bk[0], kc, hd * 128:(hd + 1) * 128],
                                                     rhs=XnT[:, kc, 0:GW], start=(kc == 0), stop=(kc == 7)),
                     reads=xres + [bk[1]], writes=["ps%d" % b2], nosync_same=True)
            s.op("act", lambda e: e.activation(out=Eq[:, 0:GW], in_=ps[:, b, 0:GW], func=AF.Exp, scale=-1.0,
                                               bias=negb[:, hd:hd + 1]),
                 reads=["ps%d" % b, "negb"], writes=["tq1"])
            s.op("act", lambda e: e.activation(out=Ek[:, 0:GW], in_=Eq[:, 0:GW], func=AF.Ln, bias=1.0),
                 reads=["tq1"], writes=["tq2"])
            s.op("dve", lambda e: e.tensor_tensor_scan(out=Lr[:, 0:GW], data0=rst[:, 0:GW], data1=Ek[:, 0:GW], initial=0.0,
                                                       op0=ALU.mult, op1=ALU.add),
                 reads=["tq2", "rst"], writes=["tq0"])
            s.op("act", lambda e: e.activation(out=Eq[:, 0:GW], in_=Lr[:, 0:GW], func=AF.Exp, scale=-1.0 / 16),
                 reads=["tq0"], writes=["tq1"])
            s.op("act", lambda e: e.activation(out=Ek[:, 0:GW], in_=Lr[:, 0:GW], func=AF.Exp, scale=1.0 / 16),
                 reads=["tq0"], writes=["tq2"])
            for t in range(NT):
                s.op("dve", lambda e, t=t: e.tensor_scalar(out=nlast[:, hd, t:t + 1], in0=Lr[:, (t + 1) * T - 1:(t + 1) * T],
                                                           scalar1=-1.0 / 16, scalar2=None, op0=ALU.mult),
                     reads=["tq0"], writes=["nlast"])
                s.op("act", lambda e, t=t: e.activation(out=Ee[:, t * T:(t + 1) * T], in_=Lr[:, t * T:(t + 1) * T],
                                                        func=AF.Exp, scale=1.0 / 16, bias=nlast[:, hd, t:t + 1]),
                     reads=["tq0", "nlast"], writes=["tq3"])
                s.op("act", lambda e, t=t: e.activation(out=dec[:, hd, t:t + 1], in_=nlast[:, hd, t:t + 1], func=AF.Exp),
                     reads=["nlast"], writes=["dec"])
            s.op("dve", lambda e: e.scalar_tensor_tensor(out=qd[:, hd, 0:GW], in0=ps[:, b1, 0:GW], scalar=128.0 ** -0.5,
                                                         in1=Eq[:, 0:GW], op0=ALU.mult, op1=ALU.mult),
                 reads=["ps%d" % b1, "tq1"], writes=rBIG(hd * 512, hd * 512 + 512))
            s.op("dve", lambda e: e.tensor_tensor(out=kd[:, hd, 0:GW], in0=ps[:, b2, 0:GW], in1=Ek[:, 0:GW], op=ALU.mult),
                 reads=["ps%d" % b2, "tq2"], writes=rBIG(2048 + hd * 512, 2048 + hd * 512 + 512))
            s.op("dve", lambda e: e.tensor_tensor(out=ke[:, hd, 0:GW], in0=ps[:, b2, 0:GW], in1=Ee[:, 0:GW], op=ALU.mult),
                 reads=["ps%d" % b2, "tq3"], writes=rBIG(4096 + hd * 512, 4096 + hd * 512 + 512))

        for hd_ in range(4):
            gate_head(hd_)
        br0 = load_blk(gwi_b, 2048, 512, ["c_gwi4"])
        br1 = load_blk(gwi_b, 2560, 512, ["c_gwi5"])

        def stA(t):
            tc0, tc1 = t * T, (t + 1) * T
            vres = rBIG(6144 + t * 1024, 6144 + t * 1024 + 1024)
            mm_tokmajor(T, lambda kc: XnT[:, kc, tc0:tc1], [xp + "%d" % t], [br0, br1], P0)
            s.op("act", lambda e: e.activation(out=srb2[:T, t % 2, :], in_=ps[:T, P0:P0 + 2, :].rearrange("p a b -> p (a b)"),
                                               func=AF.Silu),
                 reads=["ps%d" % P0, "ps%d" % (P0 + 1)], writes=["srb%d" % (t % 2)])

            def headA(hd):
                qres = rBIG(hd * 512, hd * 512 + 512)
                kres = rBIG(2048 + hd * 512, 2048 + hd * 512 + 512)
                keres = rBIG(4096 + hd * 512, 4096 + hd * 512 + 512)
                b = bankA()
                s.op("pe", lambda e: e.matmul(ps[:T, b, 0:T], lhsT=kd[:, hd, tc0:tc1], rhs=qd[:, hd, tc0:tc1], start=True, stop=True),
                     reads=qres + kres, writes=["ps%d" % b], nosync_same=True)
                b2 = bankA()
                s.op("pe", lambda e: e.transpose(out=psbf(b2)[:T, 0:128], in_=ke[:, hd, tc0:tc1], identity=ident[:, :]),
                     reads=keres + ["ident"], writes=["ps%d" % b2], nosync_same=True)
                ai = hd % 2
                s.op("dve", lambda e: e.tensor_tensor(out=attsb[:T, ai, 0:T], in0=ps[:T, b, 0:T], in1=masku[:T, 0:T], op=ALU.mult),
                     reads=["ps%d" % b, "masku"], writes=["att%d" % ai])
                s.op("act", lambda e: e.activation(out=kendt[:T, hd * 128:(hd + 1) * 128], in_=psbf(b2)[:T, 0:128], func=AF.Copy),
                     reads=["ps%d" % b2], writes=["kendt%d" % hd])

            def headB(hd):
                qres = rBIG(hd * 512, hd * 512 + 512)
                ai = hd % 2
                ob, oc = P1 + hd // 2, (hd % 2) * 256
                s.op("pe", lambda e: e.matmul(ps[:T, ob, oc:oc + 256], lhsT=qd[:, hd, tc0:tc1], rhs=Sb[:, hd, :], start=True, stop=False),
                     reads=qres + ["Sb%d" % hd], writes=["ps%d" % ob], nosync_same=True)
                s.op("pe", lambda e: e.matmul(ps[:T, ob, oc:oc + 256], lhsT=attsb[:T, ai, 0:T],
                                              rhs=vtok[:T, t, hd * 256:(hd + 1) * 256], start=False, stop=True),
                     reads=["att%d" % ai] + vres, writes=["ps%d" % ob], nosync_same=True)
                b3 = bankA()
                s.op("pe", lambda e: e.matmul(ps[:, b3, 0:256], lhsT=kendt[:T, hd * 128:(hd + 1) * 128],
                                              rhs=vtok[:T, t, hd * 256:(hd + 1) * 256], start=True, stop=True),
                     reads=["kendt%d" % hd] + vres, writes=["ps%d" % b3], nosync_same=True)
                s.op("dve", lambda e: e.scalar_tensor_tensor(out=S[:, hd, :], in0=S[:, hd, :], scalar=dec[:, hd, t:t + 1],
                                                             in1=ps[:, b3, 0:256], op0=ALU.mult, op1=ALU.add),
                     reads=["S%d" % hd, "dec", "ps%d" % b3], writes=["S%d" % hd])
                s.op("act", lambda e: e.activation(out=Sb[:, hd, :], in_=S[:, hd, :], func=AF.Copy),
                     reads=["S%d" % hd], writes=["Sb%d" % hd])

            headA(0)
            for hd_ in range(1, 4):
                headA(hd_)
                headB(hd_ - 1)
            headB(3)

        def stN(t):
            for hd in range(4):
                ob, oc = P1 + hd // 2, (hd % 2) * 256
                s.op("act", lambda e, hd=hd, ob=ob, oc=oc: e.activation(out=junk[:T, 0:256], in_=ps[:T, ob, oc:oc + 256], func=AF.Square,
                                                                        accum_out=small[:T, 44 + hd:45 + hd]),
                     reads=["ps%d" % ob], writes=["ssh%d" % hd, "junk", "sg0"])
            s.op("act", lambda e: e.activation(out=small[:T, 48:52], in_=small[:T, 44:48], func=AF.Ln, scale=1.0 / 256, bias=epsT[:T, :]),
                 reads=["ssh%d" % i for i in range(4)] + ["eps"], writes=["lnh"])
            s.op("act", lambda e: e.activation(out=small[:T, 52:56], in_=small[:T, 48:52], func=AF.Exp, scale=-0.5),
                 reads=["lnh"], writes=["rstdh"])
            for hd in range(4):
                ob, oc = P1 + hd // 2, (hd % 2) * 256
                s.op("act", lambda e, hd=hd, ob=ob, oc=oc: e.activation(out=tmpB[:T, hd * 256:(hd + 1) * 256], in_=ps[:T, ob, oc:oc + 256],
                                                                        func=AF.Copy, scale=small[:T, 52 + hd:53 + hd]),
                     reads=["ps%d" % ob, "rstdh"], writes=RB)
            s.op("dve", lambda e: e.tensor_tensor(out=tmpB[:T, :].rearrange("p (h c) -> p h c", h=4),
                                                  in0=tmpB[:T, :].rearrange("p (h c) -> p h c", h=4),
                                                  in1=ghead[:T, :].unsqueeze(1).to_broadcast([T, 4, 256]), op=ALU.mult),
                 reads=RB + ["ghead"], writes=RB)
            s.op("dve", lambda e: e.tensor_tensor(out=ob2[:T, t % 2, :], in0=tmpB[:T, :], in1=srb2[:T, t % 2, :], op=ALU.mult),
                 reads=RB + ["srb%d" % (t % 2)], writes=["ob%d" % (t % 2)])

        def stP(t):
            b, pv = transpose8(T, lambda kc: ob2[:T, t % 2, kc * 128:(kc + 1) * 128], ["ob%d" % (t % 2)])
            s.op("act", lambda e: e.activation(out=yTb[:, :, 0:T], in_=pv, func=AF.Copy), reads=["ps%d" % b], writes=["yTb"])
            mm_wout(T, lambda j: yTb[:, j, 0:T], ["yTb"], 8, P0)
            postnorm(h0 + t, T, P0, False)

        return pipeline(NT, stA, stN, stP, nxt)

    def rope(src3, res, T, nh, csT):
        x1, x2 = src3[:, :, 0:8], src3[:, :, 8:16]
        cosB = csT[:, 0:8].unsqueeze(1).to_broadcast([T, nh, 8])
        sinB = csT[:, 8:16].unsqueeze(1).to_broadcast([T, nh, 8])
        r = [rt[:T, i, 0:nh * 8].rearrange("p (h c) -> p h c", h=nh) for i in range(4)]
        s.op("dve", lambda e: e.tensor_tensor(out=r[0], in0=x1, in1=cosB, op=ALU.mult), reads=res + ["cs", "csm"], writes=["rt0"])
        s.op("dve", lambda e: e.tensor_tensor(out=r[1], in0=x2, in1=sinB, op=ALU.mult), reads=res + ["cs"], writes=["rt1"])
        s.op("dve", lambda e: e.tensor_tensor(out=r[2], in0=x2, in1=cosB, op=ALU.mult), reads=res + ["cs"], writes=["rt2"])
        s.op("dve", lambda e: e.tensor_tensor(out=r[3], in0=x1, in1=sinB, op=ALU.mult), reads=res + ["cs"], writes=["rt3"])
        s.op("dve", lambda e: e.tensor_tensor(out=x1, in0=r[0], in1=r[1], op=ALU.subtract), reads=["rt0", "rt1"], writes=res)
        s.op("dve", lambda e: e.tensor_tensor(out=x2, in0=r[2], in1=r[3], op=ALU.add), reads=["rt2", "rt3"], writes=res)

    def kv(T, NT, h0, slot0, col0, own_pre, csbuf, XnTk=XnTk, xp="XnTk", pre=None, defer=False):
        if pre is not None and "K" in os.environ.get("KDBG", ""):
            pre()
            pre = None
        if own_pre:
            own_prenorm(T, NT, h0, [(XnTk, xp, 12)])
        bw = load_blk(wkv_b, 0, 512, ["c_wkv"])

        def K1(t):
            b = bankA()
            for kc in range(8):
                s.op("pe", lambda e, kc=kc: e.matmul(ps[:T, b, :], lhsT=XnTk[:, kc, t * T:(t + 1) * T], rhs=wblk[:, bw[0], kc, :],
                                                     start=(kc == 0), stop=(kc == 7)),
                     reads=[xp + "%d" % t, bw[1]], writes=["ps%d" % b], nosync_same=True)
            s.op("act", lambda e: e.activation(out=vaug[:T, slot0 + t, :, 0:64],
                                               in_=ps[:T, b, 256:512].rearrange("p (h c) -> p h c", h=4), func=AF.Copy),
                 reads=["ps%d" % b, "vaug_init"], writes=["vaug%d" % (slot0 + t)])
            s.op("act", lambda e: e.activation(out=kf[:T, :], in_=ps[:T, b, 0:256], func=AF.Copy),
                 reads=["ps%d" % b], writes=["kf"])
            rope(kf[:T, :].rearrange("p (h c) -> p h c", h=4), ["kf"], T, 4, csbuf(t))
            for dd in range(2):
                s.op("pool", lambda e, dd=dd: e.tensor_copy(out=kdup[:T, t % 4, :, dd, :], in_=kf[:T, :].rearrange("p (h c) -> p h c", h=4)),
                     reads=["kf"], writes=["kdup%d" % (t % 4)])

        def K2(t):
            b2 = bankA()
            pv = psbf(b2)[:, 0:4 * T].rearrange("p (k t) -> p k t", k=4)
            for hd in range(4):
                s.op("pe", lambda e, hd=hd: e.transpose(out=pv[:, hd, :], in_=kdup[:T, t % 4, hd, :, :].rearrange("p a b -> p (a b)"),
                                                        identity=ident[:T, :T]),
                     reads=["kdup%d" % (t % 4), "ident"], writes=["ps%d" % b2], nosync_same=True)
            s.op("act", lambda e: e.activation(out=kT2[:, :, col0 + t * T:col0 + (t + 1) * T], in_=pv, func=AF.Copy),
                 reads=["ps%d" % b2], writes=["kT2_%d" % (slot0 + t)])

        for t_ in range(NT - 1):
            K1(t_)

        def tail():
            if pre is not None:
                pre()
            K1(NT - 1)
            for t_ in range(NT):
                K2(t_)

        if defer:
            return tail
        tail()
        return None

    def swa(first_grp, own_pre, nxt, pre=None):
        if pre is not None and "S" in os.environ.get("KDBG", ""):
            pre()
            pre = None
        T, NT = 128, 4
        if own_pre:
            own_prenorm(T, NT, 0, [(XnT, "XnT", 8)])
        load_gB(9)
        bq0 = load_blk(wq_b, 0, 512, ["c_wq"])
        bq1 = load_blk(wq_b, 512, 512, ["c_wq"])
        load_wout(swo_b, 8, ["c_swo"])

        qbuf = [(ob2[:, 0, :], "ob0"), (ob2[:, 1, :], "ob1"), (yTb[:, :, :].rearrange("p a b -> p (a b)"), "yTb")]

        def Q1(t):
            pq = pairB()
            mm_tokmajor(T, lambda kc: XnT[:, kc, t * T:(t + 1) * T], ["XnT%d" % t], [bq0, bq1], pq)
            s.op("act", lambda e: e.activation(out=tmpA[:T, :], in_=ps[:T, pq:pq + 2, :].rearrange("p a b -> p (a b)"),
                                               func=AF.Copy, scale=0.125),
                 reads=["ps%d" % pq, "ps%d" % (pq + 1)], writes=RA)
            rope(tmpA[:T, :].rearrange("p (h c) -> p h c", h=16), RA, T, 16, cs[:T, t, :])
            xi = t % 3
            s.op("act", lambda e: e.activation(out=qbuf[xi][0][:T, :], in_=tmpA[:T, :], func=AF.Copy), reads=RA, writes=[qbuf[xi][1]])
            return xi

        def Q2(t, xi):
            b, pv = transpose8(T, lambda kc: qbuf[xi][0][:T, kc * 128:(kc + 1) * 128], [qbuf[xi][1]])
            s.op("act", lambda e: e.activation(out=qT[:, :, t * T:(t + 1) * T], in_=pv, func=AF.Copy),
                 reads=["ps%d" % b], writes=["qT%d" % t] + rBIG(0, 4096))

        qx = [Q1(0), Q1(1), Q1(2)]
        Q2(0, qx[0])
        if pre is not None:
            pre()
        qx.append(Q1(3))
        Q2(1, qx[1])
        Q2(2, qx[2])
        Q2(3, qx[3])

        def stA(t):
            has_prev = not (first_grp and t == 0)
            qres = ["qT%d" % t] + rBIG(0, 4096)
            cur_c0 = 144 + t * 128
            prev_c0 = 16 if t == 0 else 144 + (t - 1) * 128
            cur_slot, prev_slot = 2 + t, (1 if t == 0 else 1 + t)

            def kvh(k):
                par = k % 2
                bc, bp, bm, bo = par, 2 + par, 4 + par, 6 + par
                s.op("pe", lambda e: e.matmul(ps[:, bc, :], lhsT=ident[:, :], rhs=mbc[:, :], start=True, stop=False),
                     reads=["ident", "mbc"], writes=["ps%d" % bc], nosync_same=True)
                if has_prev:
                    s.op("pe", lambda e: e.matmul(ps[:, bp, :], lhsT=ident[:, :], rhs=mbp[:, :], start=True, stop=False),
                         reads=["ident", "mbp"], writes=["ps%d" % bp], nosync_same=True)
                for g in range(4):
                    pr, base = 2 * k + g // 2, (g % 2) * 64
                    qv = qT[base:base + 64, pr, t * T:(t + 1) * T]
                    s.op("pe", lambda e, g=g, base=base, qv=qv: e.matmul(
                        ps[:, bc, g * 128:(g + 1) * 128], lhsT=kT2[base:base + 64, k, cur_c0:cur_c0 + 128], rhs=qv,
                        start=False, stop=(g == 3)),
                        reads=qres + ["kT2_%d" % cur_slot], writes=["ps%d" % bc], nosync_same=True)
                    if has_prev:
                        s.op("pe", lambda e, g=g, base=base, qv=qv: e.matmul(
                            ps[:, bp, g * 128:(g + 1) * 128], lhsT=kT2[base:base + 64, k, prev_c0:prev_c0 + 128], rhs=qv,
                            start=False, stop=(g == 3)),
                            reads=qres + ["kT2_%d" % prev_slot], writes=["ps%d" % bp], nosync_same=True)
                    s.op("pe", lambda e, g=g, base=base, qv=qv: e.matmul(
                        ps[:16, bm, g * 128:(g + 1) * 128], lhsT=kT2[base:base + 64, k, 0:16], rhs=qv, start=True, stop=True),
                        reads=qres + ["kT2_0"], writes=["ps%d" % bm], nosync_same=True)
                pi = par * 3
                pres = lambda i: rBIG(4096 + (pi + i) * 512, 4096 + (pi + i + 1) * 512)
                s.op("act", lambda e: e.activation(out=pT[:, pi, :], in_=ps[:, bc, :], func=AF.Exp),
                     reads=["ps%d" % bc], writes=pres(0))
                if has_prev:
                    s.op("act", lambda e: e.activation(out=pT[:, pi + 1, :], in_=ps[:, bp, :], func=AF.Exp),
                         reads=["ps%d" % bp], writes=pres(1))
                s.op("act", lambda e: e.activation(out=pT[:16, pi + 2, :], in_=ps[:16, bm, :], func=AF.Exp),
                     reads=["ps%d" % bm], writes=pres(2))
            def kvp(k):
                par = k % 2
                bo = 6 + par
                pi = par * 3
                pres = lambda i: rBIG(4096 + (pi + i) * 512, 4096 + (pi + i + 1) * 512)
                for g in range(4):
                    oc = g * 65
                    s.op("pe", lambda e, g=g, oc=oc: e.matmul(
                        ps[:, bo, oc:oc + 65], lhsT=pT[:, pi, g * 128:(g + 1) * 128], rhs=vaug[:, cur_slot, k, :], start=True, stop=False),
                        reads=pres(0) + ["vaug%d" % cur_slot], writes=["ps%d" % bo], nosync_same=True)
                    if has_prev:
                        s.op("pe", lambda e, g=g, oc=oc: e.matmul(
                            ps[:, bo, oc:oc + 65], lhsT=pT[:, pi + 1, g * 128:(g + 1) * 128], rhs=vaug[:, prev_slot, k, :], start=False, stop=False),
                            reads=pres(1) + ["vaug%d" % prev_slot], writes=["ps%d" % bo], nosync_same=True)
                    s.op("pe", lambda e, g=g, oc=oc: e.matmul(
                        ps[:, bo, oc:oc + 65], lhsT=pT[:16, pi + 2, g * 128:(g + 1) * 128], rhs=vaug[:16, 0, k, :], start=False, stop=True),
                        reads=pres(2) + ["vaug0"], writes=["ps%d" % bo], nosync_same=True)
                ov = ps[:, bo, 0:260].rearrange("p (h c) -> p h c", h=4)
                s.op("dve", lambda e: e.tensor_tensor(out=den[:, 0, :].unsqueeze(2), in0=ov[:, :, 64:65],
                                                      in1=esink[:, 4 * k:4 * k + 4].unsqueeze(2), op=ALU.add),
                     reads=["ps%d" % bo, "esink"], writes=["den"])
                s.op("dve", lambda e: e.reciprocal(out=den[:, 1, :], in_=den[:, 0, :]), reads=["den"], writes=["rden"])
                s.op("dve", lambda e: e.tensor_tensor(
                    out=ob2[:, t % 2, k * 256:(k + 1) * 256].rearrange("p (h c) -> p h c", h=4), in0=ov[:, :, 0:64],
                    in1=den[:, 1, :].unsqueeze(2).to_broadcast([128, 4, 64]), op=ALU.mult),
                    reads=["ps%d" % bo, "rden"], writes=["ob%d" % (t % 2)])

            kvh(0)
            for k_ in range(1, 4):
                kvh(k_)
                kvp(k_ - 1)
            kvp(3)

        def stN(t):
            pass

        def stP(t):
            b, pv = transpose8(T, lambda kc: ob2[:T, t % 2, kc * 128:(kc + 1) * 128], ["ob%d" % (t % 2)])
            s.op("act", lambda e: e.activation(out=yTb[:, :, 0:T], in_=pv, func=AF.Copy), reads=["ps%d" % b], writes=["yTb"])
            pb = pairB()
            mm_wout(T, lambda j: yTb[:, j, 0:T], ["yTb"], 8, pb)
            postnorm(t, T, pb, False)

        return pipeline(NT, stA, stN, stP, nxt)

    def load_x(sq, g, t):
        t0 = g * G
        s.dma("pool", lambda e: e.dma_start(out=h[:, t, :], in_=x_d[sq, t0 + t * 128:t0 + (t + 1) * 128, :]),
              "hld%d" % t, writes=["h%d" % t])

    def load_cs(g):
        t0 = g * G
        s.dma("pool", lambda e: e.dma_start(out=cs[:, :, :], in_=cs_d[NMETA + t0:NMETA + t0 + G, :].rearrange("(t p) c -> p t c", p=128)),
              "cs", writes=["cs"])

    def store_y(sq, g, t):
        t0 = g * G
        s.dma("pool", lambda e: e.dma_start(out=y_d[sq, t0 + t * 128:t0 + (t + 1) * 128, :], in_=h[:, t, :]),
              "hst%d" % t, reads=["h%d" % t])

    def carry_kv():
        s.op("pool", lambda e: e.tensor_copy(out=kT2[:, :, 16:144], in_=kT2[:, :, 528:656]),
             reads=["kT2_5"], writes=["kT2_1"])
        s.op("pool", lambda e: e.tensor_copy(out=vaug[:, 1, :, 0:64], in_=vaug[:, 5, :, 0:64]),
             reads=["vaug5"], writes=["vaug1"])

    groups = [(sq, g) for sq in range(n_seq) for g in range(n_grp)]

    def boundary_nxt(i):
        sq, g = groups[i]
        if i + 1 < len(groups):
            nsq, ng = groups[i + 1]

            def before(t):
                store_y(sq, g, t)
                load_x(nsq, ng, t)
            return Nxt([(XnT, "XnT", 0)], before)
        return None

    XN = lambda gi: Nxt([(XnT, "XnT", gi)])

    s.dma("pool", lambda e: e.dma_start(out=h[:16, 4, :], in_=meta_d), "hld4", writes=["h4"])
    for t in range(4):
        load_x(0, 0, t)
    load_cs(0)
    cast_ffn(0)
    cast_gla()
    d = ffn(0, 0, 1, 128, 4, 0, True, XN(2))
    cast_ffn(1)
    cast_kv()
    ffn(0, 0, 1, 16, 1, 4, True, None, XnT=XnTm, xp="XnTm")
    load_gla_consts()
    gla(16, 1, 4, True, None, XnT=XnTm, xp="XnTm")
    for hd in range(4):
        s.op("act", lambda e, hd=hd: e.activation(out=Sm[:, hd, :], in_=S[:, hd, :], func=AF.Copy), reads=["S%d" % hd], writes=["Sm%d" % hd])
        s.op("pool", lambda e, hd=hd: e.tensor_copy(out=Smb[:, hd, :], in_=Sb[:, hd, :]), reads=["Sb%d" % hd], writes=["Smb%d" % hd])
    cast_ffn(2)
    d = gla(128, 4, 0, False, XN(4), pre=d)
    ffn(1, 4, 5, 16, 1, 4, True, None, XnT=XnTm, xp="XnTm")
    cast_swa()
    cast_ffn(3)
    d = ffn(1, 4, 5, 128, 4, 0, False, Nxt([(XnTk, "XnTk", 12), (XnT, "XnT", 6)]), pre=d)
    kv(16, 1, 4, 0, 0, True, lambda t: csm[:16, :], XnTk=XnTkm, xp="XnTkm")
    kv(128, 4, 0, 2, 144, False, lambda t: cs[:128, t, :], pre=d)
    d = ffn(2, 6, 7, 128, 4, 0, False, XN(8))
    d = swa(True, False, XN(10), pre=d)
    d = ffn(3, 10, 11, 128, 4, 0, False, boundary_nxt(0), pre=d)
    if len(groups) == 1:
        for t in range(4):
            store_y(0, 0, t)
    for i in range(1, len(groups)):
        sq, g = groups[i]
        if g == 0:
            for hd in range(4):
                s.op("act", lambda e, hd=hd: e.activation(out=S[:, hd, :], in_=Sm[:, hd, :], func=AF.Copy), reads=["Sm%d" % hd], writes=["S%d" % hd])
                s.op("pool", lambda e, hd=hd: e.tensor_copy(out=Sb[:, hd, :], in_=Smb[:, hd, :]), reads=["Smb%d" % hd], writes=["Sb%d" % hd])
        else:
            carry_kv()
        load_cs(g)
        d = ffn(0, 0, 1, 128, 4, 0, False, XN(2), pre=d)
        d = gla(128, 4, 0, False, XN(4), pre=d)
        d = ffn(1, 4, 5, 128, 4, 0, False, Nxt([(XnTk, "XnTk", 12), (XnT, "XnT", 6)]), pre=d)
        kv(128, 4, 0, 2, 144, False, lambda t: cs[:128, t, :], pre=d)
        d = ffn(2, 6, 7, 128, 4, 0, False, XN(8))
        d = swa(g == 0, False, XN(10), pre=d)
        nb = boundary_nxt(i)
        d = ffn(3, 10, 11, 128, 4, 0, False, nb, pre=d)
        if nb is None:
            for t in range(4):
                store_y(sq, g, t)
    s.emit()
    st.close()
    return nc


def host_inputs(inp):
    f = lambda a: np.ascontiguousarray(np.asarray(a, dtype=np.float32))
    gains = np.concatenate([f(inp["norm_gains"]).reshape(12, D), f(inp["kv_norm"]).reshape(1, D)], 0)
    gT = gains.reshape(13, 8, 128).transpose(2, 0, 1).reshape(128, 13 * 8)
    bf = ml_dtypes.bfloat16
    i = np.arange(128)
    masku = (i[None, :] >= i[:, None]).astype(np.float32)
    mb_cur = np.where(i[:, None] <= i[None, :], 0.0, NEG).astype(np.float32)
    mb_prev = np.where(i[:, None] > i[None, :], 0.0, NEG).astype(np.float32)
    rstm = np.ones((128, 512), np.float32)
    rstm[:, ::128] = 0.0
    inv = (500000.0 ** (-np.arange(0, 16, 2, dtype=np.float32) / 16)).astype(np.float32)
    ang = np.arange(NMETA + SEQ, dtype=np.float32)[:, None] * inv[None, :]
    cs = np.concatenate([np.cos(ang), np.sin(ang)], 1).astype(np.float32)
    common = {
        "meta": f(inp["meta_tokens"]), "gT": np.ascontiguousarray(gT), "gains": gains,
        "w_ffn_in": f(inp["w_ffn_in"]).reshape(4, D, 2 * DFF), "w_ffn_out": f(inp["w_ffn_out"]).reshape(4, DFF, D),
        "gla_w_in": f(inp["gla_w_in"])[0], "gla_w_gate": f(inp["gla_w_gate"])[0],
        "gla_b4": np.ascontiguousarray(f(inp["gla_b_gate"])[0].reshape(4, 128).T),
        "gla_norm": f(inp["gla_norm"]).reshape(1, 256), "gla_w_out": f(inp["gla_w_out"])[0],
        "w_kv": f(inp["w_kv"]), "swa_w_q": f(inp["swa_w_q"])[0], "swa_sinks": f(inp["swa_sinks"]).reshape(1, 16),
        "swa_w_out": f(inp["swa_w_out"])[0],
        "ident": np.eye(128, dtype=np.float32).astype(bf), "masku": masku.astype(bf),
        "mb_cur": np.tile(mb_cur, (1, 4)).astype(bf), "mb_prev": np.tile(mb_prev, (1, 4)).astype(bf),
        "rstmask": rstm.astype(bf), "cs": cs,
    }
    return common


_NC_CACHE = {}


def kernel(**inputs):
    common = host_inputs(inputs)
    x = np.ascontiguousarray(np.asarray(inputs["x"], dtype=np.float32))
    if "nc" not in _NC_CACHE:
        _NC_CACHE["nc"] = build()
    nc = _NC_CACHE["nc"]
    in_maps = [dict(common, x=x[2 * c:2 * c + 2]) for c in range(8)]
    res = run_bass_kernel_spmd(nc, in_maps, core_ids=list(range(8)))
    return np.concatenate([np.asarray(r["y"], dtype=np.float32) for r in res.results], axis=0)
```
